# Optimizing a Trainium2 kernel written in Bass

```python
import jax, jax.numpy as jnp
from jax import lax
import numpy as np

D_MODEL = 1024
BATCH = 8
SEQ = 2048
DEPTH = 2

CHUNK = 64
HEAD_DIM = 64
N_HEADS_SB = 4
N_HEADS_CH = 8
N_HEADS_FOX = 4
W_SB = N_HEADS_SB * HEAD_DIM
W_CH = N_HEADS_CH * HEAD_DIM
W_FOX = N_HEADS_FOX * HEAD_DIM
LEFT_CHUNKS = 8
BAND = (LEFT_CHUNKS + 1) * CHUNK
MAX_REL = 128
N_REL = 2 * MAX_REL + 1
Q_BLOCK = 128
N_BRANCH = 3
D_FF = ((8 * D_MODEL // 3 + 127) // 128) * 128
QKV_WIDTH = 3 * (W_SB + W_CH + W_FOX)
FORGET_OFFSET = QKV_WIDTH
IN_WIDTH = QKV_WIDTH + N_HEADS_FOX + N_BRANCH * D_MODEL
SPLIT_SIZES = (W_SB, W_SB, W_SB, W_CH, W_CH, W_CH, W_FOX, W_FOX, W_FOX, N_HEADS_FOX, D_MODEL, D_MODEL, D_MODEL)
SPLIT_POINTS = tuple(int(v) for v in np.cumsum(SPLIT_SIZES)[:-1])
FORGET_BIAS_INIT = 4.0
RMS_EPS = 1e-6
NEG = -1e30

kernel_name = "hybrid_stickbreak_chunkrel_fox_macaron"


def rmsnorm(x, g):
    xf = x.astype(jnp.float32)
    y = xf * lax.rsqrt(jnp.mean(xf * xf, axis=-1, keepdims=True) + RMS_EPS)
    return (y * g.astype(jnp.float32)).astype(x.dtype)


def swiglu(h, w_in, w_out):
    gate, up = jnp.split(h @ w_in, 2, axis=-1)
    return (jax.nn.silu(gate) * up) @ w_out


def split_heads(t, n_heads):
    b, s, _ = t.shape
    return t.reshape(b, s, n_heads, HEAD_DIM).transpose(0, 2, 1, 3)


def merge_heads(t):
    b, h, s, d = t.shape
    return t.transpose(0, 2, 1, 3).reshape(b, s, h * d)


def stick_breaking_attention(q, k, v):
    T = q.shape[2]
    scale = HEAD_DIM ** -0.5
    outs = []
    for start in range(0, T, Q_BLOCK):
        end = start + Q_BLOCK
        z = jnp.einsum('bhqd,bhkd->bhqk', q[:, :, start:end], k[:, :, :end]).astype(jnp.float32) * scale
        strict = jnp.arange(end)[None, :] < jnp.arange(start, end)[:, None]
        log_beta = jax.nn.log_sigmoid(z)
        log_fail = jnp.where(strict, jax.nn.log_sigmoid(-z), 0.0)
        between = lax.cumsum(log_fail, axis=3, reverse=True) - log_fail
        w = jnp.where(strict, jnp.exp(log_beta + between), 0.0)
        outs.append(jnp.einsum('bhqk,bhkd->bhqd', w.astype(v.dtype), v[:, :, :end]))
    return jnp.concatenate(outs, axis=2)


def forgetting_attention(q, k, v, log_f):
    T = q.shape[2]
    scale = HEAD_DIM ** -0.5
    F = jnp.cumsum(log_f, axis=-1)
    outs = []
    for start in range(0, T, Q_BLOCK):
        end = start + Q_BLOCK
        z = jnp.einsum('bhqd,bhkd->bhqk', q[:, :, start:end], k[:, :, :end]).astype(jnp.float32) * scale
        z = z + F[:, :, start:end, None] - F[:, :, None, :end]
        causal = jnp.arange(end)[None, :] <= jnp.arange(start, end)[:, None]
        p = jax.nn.softmax(jnp.where(causal, z, NEG), axis=-1)
        outs.append(jnp.einsum('bhqk,bhkd->bhqd', p.astype(v.dtype), v[:, :, :end]))
    return jnp.concatenate(outs, axis=2)


def chunked_relpos_attention(q, k, v, rel_table):
    B, H, T, Dh = q.shape
    nc = T // CHUNK
    scale = Dh ** -0.5
    qc = q.reshape(B, H, nc, CHUNK, Dh)
    pad = ((0, 0), (0, 0), (LEFT_CHUNKS * CHUNK, 0), (0, 0))
    kp = jnp.pad(k, pad).reshape(B, H, nc + LEFT_CHUNKS, CHUNK, Dh)
    vp = jnp.pad(v, pad).reshape(B, H, nc + LEFT_CHUNKS, CHUNK, Dh)
    band_idx = jnp.arange(nc)[:, None] + jnp.arange(LEFT_CHUNKS + 1)[None, :]
    k_band = kp[:, :, band_idx].reshape(B, H, nc, BAND, Dh)
    v_band = vp[:, :, band_idx].reshape(B, H, nc, BAND, Dh)
    z = jnp.einsum('bhcqd,bhckd->bhcqk', qc, k_band).astype(jnp.float32) * scale
    rel = (jnp.arange(CHUNK)[:, None] + LEFT_CHUNKS * CHUNK) - jnp.arange(BAND)[None, :]
    rel = jnp.clip(rel, -MAX_REL, MAX_REL) + MAX_REL
    bias = rel_table[rel].astype(jnp.float32).transpose(2, 0, 1)
    z = z + bias[None, :, None]
    key_abs = (jnp.arange(nc)[:, None] - LEFT_CHUNKS) * CHUNK + jnp.arange(BAND)[None, :]
    valid = key_abs >= 0
    p = jax.nn.softmax(jnp.where(valid[None, None, :, None, :], z, NEG), axis=-1)
    o = jnp.einsum('bhcqk,bhckd->bhcqd', p.astype(v.dtype), v_band)
    return o.reshape(B, H, T, Dh)


def hybrid_layer(x, g_ffn1, w_ffn1_in, w_ffn1_out, g_mix, w_in, b_in, rel_bias,
                 w_br_sb, w_br_ch, w_br_fox, w_out, g_ffn2, w_ffn2_in, w_ffn2_out):
    x = x + 0.5 * swiglu(rmsnorm(x, g_ffn1), w_ffn1_in, w_ffn1_out)
    h = rmsnorm(x, g_mix)
    proj = h @ w_in + b_in
    (q_a, k_a, v_a, q_b, k_b, v_b, q_c, k_c, v_c,
     f_logit, g_a, g_b, g_c) = jnp.split(proj, list(SPLIT_POINTS), axis=-1)
    o_a = stick_breaking_attention(split_heads(q_a, N_HEADS_SB), split_heads(k_a, N_HEADS_SB),
                                   split_heads(v_a, N_HEADS_SB))
    o_b = chunked_relpos_attention(split_heads(q_b, N_HEADS_CH), split_heads(k_b, N_HEADS_CH),
                                   split_heads(v_b, N_HEADS_CH), rel_bias)
    log_f = jax.nn.log_sigmoid(f_logit.astype(jnp.float32)).transpose(0, 2, 1)
    o_c = forgetting_attention(split_heads(q_c, N_HEADS_FOX), split_heads(k_c, N_HEADS_FOX),
                               split_heads(v_c, N_HEADS_FOX), log_f)
    merged = (jax.nn.sigmoid(g_a) * (merge_heads(o_a) @ w_br_sb)
              + jax.nn.sigmoid(g_b) * (merge_heads(o_b) @ w_br_ch)
              + jax.nn.sigmoid(g_c) * (merge_heads(o_c) @ w_br_fox))
    x = x + merged @ w_out
    x = x + 0.5 * swiglu(rmsnorm(x, g_ffn2), w_ffn2_in, w_ffn2_out)
    return x


def setup_inputs(seed: int = 0) -> dict:
    key = jax.random.key(seed)
    ks = jax.random.split(key, 18)

    def dense(k, shape, fan_in):
        return jax.random.normal(k, shape, jnp.float32) * fan_in ** -0.5

    def gain(k, shape):
        return 1.0 + 0.05 * jax.random.normal(k, shape, jnp.float32)

    b_in = 0.02 * jax.random.normal(ks[6], (DEPTH, IN_WIDTH), jnp.float32)
    b_in = b_in.at[:, FORGET_OFFSET:FORGET_OFFSET + N_HEADS_FOX].add(FORGET_BIAS_INIT)
    return {
        "x": jax.random.normal(ks[0], (BATCH, SEQ, D_MODEL), jnp.float32),
        "g_ffn1": gain(ks[1], (DEPTH, D_MODEL)),
        "w_ffn1_in": dense(ks[2], (DEPTH, D_MODEL, 2 * D_FF), D_MODEL),
        "w_ffn1_out": dense(ks[3], (DEPTH, D_FF, D_MODEL), D_FF),
        "g_mix": gain(ks[4], (DEPTH, D_MODEL)),
        "w_in": dense(ks[5], (DEPTH, D_MODEL, IN_WIDTH), D_MODEL),
        "b_in": b_in,
        "rel_bias": 0.1 * jax.random.normal(ks[7], (DEPTH, N_REL, N_HEADS_CH), jnp.float32),
        "w_br_sb": dense(ks[8], (DEPTH, W_SB, D_MODEL), W_SB),
        "w_br_ch": dense(ks[9], (DEPTH, W_CH, D_MODEL), W_CH),
        "w_br_fox": dense(ks[10], (DEPTH, W_FOX, D_MODEL), W_FOX),
        "w_out": dense(ks[11], (DEPTH, D_MODEL, D_MODEL), D_MODEL),
        "g_ffn2": gain(ks[12], (DEPTH, D_MODEL)),
        "w_ffn2_in": dense(ks[13], (DEPTH, D_MODEL, 2 * D_FF), D_MODEL),
        "w_ffn2_out": dense(ks[14], (DEPTH, D_FF, D_MODEL), D_FF),
        "g_final": gain(ks[15], (D_MODEL,)),
    }


def reference(x, g_ffn1, w_ffn1_in, w_ffn1_out, g_mix, w_in, b_in, rel_bias,
              w_br_sb, w_br_ch, w_br_fox, w_out, g_ffn2, w_ffn2_in, w_ffn2_out, g_final):
    for layer in range(DEPTH):
        x = hybrid_layer(x, g_ffn1[layer], w_ffn1_in[layer], w_ffn1_out[layer], g_mix[layer],
                         w_in[layer], b_in[layer], rel_bias[layer], w_br_sb[layer], w_br_ch[layer],
                         w_br_fox[layer], w_out[layer], g_ffn2[layer], w_ffn2_in[layer],
                         w_ffn2_out[layer])
    return rmsnorm(x, g_final)
```

```python
import contextlib
from collections import defaultdict

import numpy as np
import concourse.bass as bass
import concourse.mybir as mybir
from concourse.bass_utils import run_bass_kernel_spmd

F32 = mybir.dt.float32
BF16 = mybir.dt.bfloat16
AF = mybir.ActivationFunctionType
ALU = mybir.AluOpType

D = 1024
NCH = 8
T = 2048
TG = 512
NTG = 4
DFF = 2816
NF = 22
INW = 6148
QCOL = [0, 128, 768, 896, 1024, 1152, 2304, 2432]
KCOL = [256, 384, 1280, 1408, 1536, 1664, 2560, 2688]
VA, VB, VC = 512, 1792, 2816
FOFF = 3072
GCOL = [3076, 4100, 5124]
NEG = -30000.0
EXT = 768


class View:
    __slots__ = ("ap", "reg")

    def __init__(self, ap, reg):
        self.ap = ap
        self.reg = reg


class Tile:
    def __init__(self, ap, space, base, shape, esize):
        self.ap = ap
        self.space = space
        self.base = base
        self.shape = tuple(shape)
        self.esize = esize
        st = []
        s = 1
        for d in reversed(self.shape):
            st.append(s)
            s *= d
        self.strides = tuple(reversed(st))
        self.nbytes = s * esize

    def __getitem__(self, idx):
        if not isinstance(idx, tuple):
            idx = (idx,)
        idx = idx + (slice(None),) * (1 + len(self.shape) - len(idx))
        p = idx[0]
        p0, p1, _ = p.indices(128)
        lo = hi = 0
        for dim, stv, ix in zip(self.shape, self.strides, idx[1:]):
            if isinstance(ix, int):
                lo += ix * stv
                hi += ix * stv
            else:
                a, b, _ = ix.indices(dim)
                lo += a * stv
                hi += (b - 1) * stv
        b0 = self.base + lo * self.esize
        b1 = self.base + (hi + 1) * self.esize
        return View(self.ap[idx], (self.space, p0, p1, b0, b1))


class Sched:
    PAGE = 1024

    def __init__(self):
        self.ops = []
        self.eng_ops = defaultdict(list)
        self.pages = defaultdict(set)
        self.recs = {}
        self.nrec = 0
        self.dma_count = defaultdict(int)

    def _cands(self, reg):
        sp, p0, p1, b0, b1 = reg
        out = set()
        for pg in range(b0 // self.PAGE, (b1 - 1) // self.PAGE + 1):
            out |= self.pages.get((sp, pg), set())
        res = []
        for rid in out:
            r = self.recs[rid]
            _, q0, q1, c0, c1 = r[0]
            if q0 < p1 and p0 < q1 and c0 < b1 and b0 < c1:
                res.append(rid)
        return res

    def _newrec(self, reg, writer, readers):
        rid = self.nrec
        self.nrec += 1
        self.recs[rid] = [reg, writer, readers]
        sp, p0, p1, b0, b1 = reg
        for pg in range(b0 // self.PAGE, (b1 - 1) // self.PAGE + 1):
            self.pages[(sp, pg)].add(rid)
        return rid

    def _delrec(self, rid):
        reg = self.recs.pop(rid)[0]
        sp, p0, p1, b0, b1 = reg
        for pg in range(b0 // self.PAGE, (b1 - 1) // self.PAGE + 1):
            self.pages[(sp, pg)].discard(rid)

    @staticmethod
    def _ovl(a, b):
        return a[1] < b[2] and b[1] < a[2] and a[3] < b[4] and b[3] < a[4]

    @staticmethod
    def _contains(a, b):
        return a[1] <= b[1] and b[2] <= a[2] and a[3] <= b[3] and b[4] <= a[4]

    @staticmethod
    def _remainder(r, w):
        sp, p0, p1, b0, b1 = r
        _, q0, q1, c0, c1 = w
        out = []
        lo = max(b0, c0)
        hi = min(b1, c1)
        if b0 < lo:
            out.append((sp, p0, p1, b0, lo))
        if hi < b1:
            out.append((sp, p0, p1, hi, b1))
        if p0 < q0:
            out.append((sp, p0, min(p1, q0), lo, hi))
        if q1 < p1:
            out.append((sp, max(p0, q1), p1, lo, hi))
        return out

    def add(self, eng, fn, reads=(), writes=(), dma=None, ndma=1):
        gid = len(self.ops)
        deps = set()
        for reg in reads:
            covered = False
            for rid in self._cands(reg):
                r = self.recs[rid]
                if r[1] is not None:
                    deps.add(r[1])
                r[2].append((gid, reg))
                if self._contains(r[0], reg):
                    covered = True
            if not covered:
                self._newrec(reg, None, [(gid, reg)])
        for reg in writes:
            for rid in self._cands(reg):
                r = self.recs[rid]
                if r[1] is not None:
                    deps.add(r[1])
                for (g2, rr) in r[2]:
                    if self._ovl(rr, reg):
                        deps.add(g2)
                pieces = self._remainder(r[0], reg)
                self._delrec(rid)
                for pc in pieces:
                    self._newrec(pc, r[1], [(g2, rr) for (g2, rr) in r[2] if self._ovl(rr, pc)])
            self._newrec(reg, gid, [])
        deps.discard(gid)
        op = dict(eng=eng, fn=fn, deps=deps, dma=dma, ndma=ndma, signal=False, local=len(self.eng_ops[eng]))
        if dma is not None:
            self.dma_count[dma] += ndma
            op["dmaval"] = 16 * self.dma_count[dma]
        if eng == "pe":
            op["deps"] = {d for d in deps if not (self.ops[d]["eng"] == "pe" and self.ops[d]["dma"] is None)}
        for d in op["deps"]:
            self.ops[d]["signal"] = True
        self.ops.append(op)
        self.eng_ops[eng].append(gid)
        return gid

    def finalize(self):
        for eng, lst in self.eng_ops.items():
            cnt = 0
            for gid in lst:
                op = self.ops[gid]
                if op["dma"] is None and op["signal"]:
                    cnt += 1
                    op["sigval"] = cnt

    def emit_engine(self, eng, handle, sems):
        seen = {}
        for gid in self.eng_ops[eng]:
            op = self.ops[gid]
            need = {}
            for d in op["deps"]:
                pd = self.ops[d]
                if pd["dma"] is not None:
                    key, val = "dma:" + pd["dma"], pd["dmaval"]
                else:
                    key, val = "eng:" + pd["eng"], pd["sigval"]
                if need.get(key, 0) < val:
                    need[key] = val
            for key, val in need.items():
                if seen.get(key, 0) >= val:
                    continue
                handle.wait_ge(sems[key], val)
                seen[key] = val
            insts = op["fn"](handle)
            if op["dma"] is not None:
                if not isinstance(insts, (list, tuple)):
                    insts = [insts]
                assert len(insts) == op["ndma"]
                for ins in insts:
                    ins.then_inc(sems["dma:" + op["dma"]], 16)
            elif op["signal"]:
                if isinstance(insts, (list, tuple)):
                    insts = insts[-1]
                insts.then_inc(sems["eng:" + eng], 1)


class Builder:
    def __init__(self, n_layers=2, dbg=None, stop_after=None):
        self.n_layers = n_layers
        self.dbg = dbg or {}
        self.stop_after = stop_after
        self.S = Sched()
        self.dma_sems = []
        self.out_sems = []
        self.dbg_out = {}

    def arena_tile(self, off, shape, dtype):
        es = 4 if dtype == F32 else 2
        n = 1
        for d in shape:
            n *= d
        nbytes = n * es
        assert off % 4 == 0
        assert off + nbytes <= self.arena_bytes, (off, nbytes, self.arena_bytes)
        ap = self.arena[:, off // 2:(off + nbytes) // 2]
        if dtype == F32:
            ap = ap.bitcast(F32)
        if len(shape) == 2:
            ap = ap.rearrange("p (a b) -> p a b", a=shape[0])
        elif len(shape) == 3:
            ap = ap.rearrange("p (a b c) -> p a b c", a=shape[0], b=shape[1])
        return Tile(ap, "sb", off, shape, es)

    def alloc(self, shape, dtype):
        es = 4 if dtype == F32 else 2
        n = 1
        for d in shape:
            n *= d
        nbytes = (n * es + 31) // 32 * 32
        t = self.arena_tile(self.cur, shape, dtype)
        self.cur += nbytes
        self.peak = max(self.peak, self.cur)
        return t

    def newsem(self, name):
        self.dma_sems.append(name)
        return name

    def mm(self, out, lhsT, rhs, start, stop):
        self.S.add("pe", lambda e: e.matmul(out.ap, lhsT.ap, rhs.ap, start=start, stop=stop),
                   reads=[lhsT.reg, rhs.reg], writes=[out.reg])

    def tr(self, out, in_, ident):
        self.S.add("pe", lambda e: e.transpose(out.ap, in_.ap, ident.ap),
                   reads=[in_.reg, ident.reg], writes=[out.reg])

    def act(self, out, in_, func, bias=None, scale=1.0):
        reads = [in_.reg]
        if isinstance(bias, View):
            reads.append(bias.reg)
            b = bias.ap
        elif bias is None:
            b = 0.0
        else:
            b = bias
        self.S.add("act", lambda e: e.activation(out=out.ap, in_=in_.ap, func=func, bias=b, scale=scale),
                   reads=reads, writes=[out.reg])

    def tt(self, out, in0, in1, op, eng="dve"):
        self.S.add(eng, lambda e: e.tensor_tensor(out=out.ap, in0=in0.ap, in1=in1.ap, op=op),
                   reads=[in0.reg, in1.reg], writes=[out.reg])

    def ts(self, out, in0, s1, s2, op0, op1=None, eng="dve"):
        reads = [in0.reg]
        a1 = s1
        a2 = s2
        if isinstance(s1, View):
            reads.append(s1.reg)
            a1 = s1.ap
        if isinstance(s2, View):
            reads.append(s2.reg)
            a2 = s2.ap
        if op1 is None:
            self.S.add(eng, lambda e: e.tensor_scalar(out=out.ap, in0=in0.ap, scalar1=a1, scalar2=None, op0=op0),
                       reads=reads, writes=[out.reg])
        else:
            self.S.add(eng, lambda e: e.tensor_scalar(out=out.ap, in0=in0.ap, scalar1=a1, scalar2=a2, op0=op0, op1=op1),
                       reads=reads, writes=[out.reg])

    def stt(self, out, in0, scalar, in1, op0, op1):
        reads = [in0.reg, in1.reg]
        a = scalar
        if isinstance(scalar, View):
            reads.append(scalar.reg)
            a = scalar.ap
        self.S.add("dve", lambda e: e.scalar_tensor_tensor(out=out.ap, in0=in0.ap, scalar=a, in1=in1.ap, op0=op0, op1=op1),
                   reads=reads, writes=[out.reg])

    def copy(self, out, in_, eng="dve"):
        if eng == "act":
            self.act(out, in_, AF.Copy)
        else:
            self.S.add(eng, lambda e: e.tensor_copy(out=out.ap, in_=in_.ap), reads=[in_.reg], writes=[out.reg])

    def recip(self, out, in_):
        self.S.add("dve", lambda e: e.reciprocal(out=out.ap, in_=in_.ap), reads=[in_.reg], writes=[out.reg])

    def memset(self, out, val, eng="dve"):
        self.S.add(eng, lambda e: e.memset(out.ap, val), writes=[out.reg])

    def aselect(self, out, in_, pattern, base, cm, cmp, fill):
        self.S.add("pool", lambda e: e.affine_select(out=out.ap, in_=in_.ap, pattern=pattern, base=base,
                                                     channel_multiplier=cm, compare_op=cmp, fill=fill),
                   reads=[in_.reg], writes=[out.reg])

    def dma_in(self, queue, out, srcs, sem):
        def fn(e):
            return [e.dma_start(out=o.ap, in_=s) for o, s in srcs]
        self.S.add(queue, fn, writes=[out.reg], dma=sem, ndma=len(srcs))

    def dma_out(self, queue, dst_ap, src, sem):
        self.S.add(queue, lambda e: [e.dma_start(out=dst_ap, in_=src.ap)], reads=[src.reg], dma=sem, ndma=1)

    def slot(self):
        i = self.slot_i % self.nslots
        self.slot_i += 1
        return self.slots[i], self.slot_sems[i]

    def load_cols(self, wl, col0, ncol=128, nch=NCH):
        sl, sem = self.slot()
        v = sl[:, 0:nch, 0:ncol]
        self.dma_in("pool", v, [(v, wl[:, col0:col0 + ncol].rearrange("(c p) n -> p c n", p=128))], sem)
        return sl

    def setup_consts(self):
        c = self
        c.ident_f = c.alloc([128], F32)
        c.ones_f = c.alloc([128], F32)
        c.triu_f = c.alloc([128], F32)
        c.ident_b = c.alloc([128], BF16)
        c.J_b = c.alloc([128], BF16)
        c.ones_b = c.alloc([128], BF16)
        c.negU_b = c.alloc([128], BF16)
        c.negones_b = c.alloc([128], BF16)
        c.maskA_b = c.alloc([128], BF16)
        c.maskC_b = c.alloc([128], BF16)
        c.zeros_b = c.alloc([512], BF16)
        c.eps = c.alloc([1], F32)
        c.gT = c.alloc([56], F32)
        c.bfm = c.alloc([2, 40], F32)
        pat = [[1, 128]]
        c.memset(c.ones_f[:], 1.0, "pool")
        c.aselect(c.ident_f[:], c.ones_f[:], pat, 0, -1, ALU.is_equal, 0.0)
        c.aselect(c.triu_f[:], c.ones_f[:], pat, 0, -1, ALU.is_ge, 0.0)
        c.memset(c.ones_b[:], 1.0, "pool")
        c.memset(c.negones_b[:], -1.0, "pool")
        c.memset(c.zeros_b[:], 0.0, "pool")
        c.memset(c.eps[:], 1e-6, "pool")
        c.aselect(c.ident_b[:], c.ones_b[:], pat, 0, -1, ALU.is_equal, 0.0)
        c.aselect(c.J_b[:], c.ones_b[:], pat, -127, 1, ALU.is_equal, 0.0)
        c.aselect(c.negU_b[:], c.negones_b[:], [[-1, 128]], 0, 1, ALU.is_ge, 0.0)
        c.aselect(c.maskA_b[:], c.zeros_b[:, 0:128], pat, 0, -1, ALU.is_gt, NEG)
        c.aselect(c.maskC_b[:], c.zeros_b[:, 0:128], pat, 0, -1, ALU.is_ge, NEG)
        sem = c.newsem("vec")
        c.dma_in("sp", c.gT[:], [(c.gT[:], c.d["gT"])], sem)
        sem = c.newsem("bfm")
        c.dma_in("sp", c.bfm[:], [(c.bfm[:], c.d["bfm"].rearrange("l p n -> p l n"))], sem)

    def load_x(self):
        c = self
        save = c.cur
        xs = [c.alloc([4, D], F32) for _ in range(2)]
        sems = [c.newsem("xs0"), c.newsem("xs1")]
        k = 0
        for tg in range(NTG):
            st = xs[tg % 2]
            c.dma_in("sp", st[:], [(st[:], c.d["x"][tg * TG:(tg + 1) * TG, :].rearrange("(b p) d -> p b d", p=128))], sems[tg % 2])
            for ch in range(NCH):
                bank = c.ps[k % 8]
                for b in range(4):
                    c.tr(bank[:, b * 128:(b + 1) * 128], st[:, b, ch * 128:(ch + 1) * 128], c.ident_f[:])
                dst = c.XT[:, ch, tg * TG:(tg + 1) * TG]
                if k % 2 == 0:
                    c.copy(dst, bank[:], "dve")
                else:
                    c.copy(dst, bank[:], "act")
                k += 1
        c.cur = save

    def norm_stats(self, tg, bank, tmp):
        c = self
        sq, lnt, rstd = tmp
        for ch in range(NCH):
            s = sq[ch % 2]
            c.act(s[:], c.XT[:, ch, tg * TG:(tg + 1) * TG], AF.Square)
            c.mm(bank[:], c.ones_b[:], s[:], ch == 0, ch == NCH - 1)
        c.act(lnt[:], bank[:], AF.Ln, bias=c.eps[:, 0:1], scale=1.0 / D)
        c.act(rstd[:], lnt[:], AF.Exp, scale=-0.5)
        return rstd

    def norm_to(self, tg, gidx, dst_fn, bank, tmp):
        c = self
        rstd = c.norm_stats(tg, bank, tmp)
        for ch in range(NCH):
            c.stt(dst_fn(ch), c.XT[:, ch, tg * TG:(tg + 1) * TG], c.gT[:, gidx * 8 + ch:gidx * 8 + ch + 1], rstd[:],
                  ALU.mult, ALU.mult)

    def alloc_norm_tmp(self):
        c = self
        return ([c.alloc([TG], BF16), c.alloc([TG], BF16)], c.alloc([TG], F32), c.alloc([TG], F32))

    def ffn(self, l, w_in_d, w_out_d, gidx):
        c = self
        save = c.cur
        hT = c.alloc([NCH, 2 * TG], BF16)
        actT = c.alloc([NF, 2 * TG], BF16)
        WO = c.alloc([NF, D], BF16)
        tmp = c.alloc_norm_tmp()
        sqx = list(tmp[0]) + [c.alloc([TG], BF16) for _ in range(6)]
        sil = [c.alloc([TG], F32) for _ in range(2)]
        big = [(c.arena_tile(c.slots[2 * i].base, [NCH, 256], BF16), c.slot_sems[2 * i]) for i in range(4)]
        wl_in = w_in_d[l]
        wl_out = w_out_d[l]
        st = dict(ri=0)

        def load2(col0):
            sl, sem = big[st["ri"] % 4]
            st["ri"] += 1
            c.dma_in("pool", sl[:], [(sl[:], wl_in[:, col0:col0 + 256].rearrange("(c p) n -> p c n", p=128))], sem)
            return sl

        def hoisted_norm_piece(m):
            rs = [tmp[1], tmp[2]]
            if m in (0, 1):
                tg = 2 + m
                for ch in range(NCH):
                    c.act(sqx[ch][:], c.XT[:, ch, tg * TG:(tg + 1) * TG], AF.Square)
                    c.mm(c.ps[6 + m][:], c.ones_b[:], sqx[ch][:], ch == 0, ch == NCH - 1)
            elif m == 2:
                for t2 in range(2):
                    c.act(rs[t2][:], c.ps[6 + t2][:], AF.Ln, bias=c.eps[:, 0:1], scale=1.0 / D)
                    c.act(rs[t2][:], rs[t2][:], AF.Exp, scale=-0.5)
            elif m <= 6:
                for i in range(4):
                    idx = (m - 3) * 4 + i
                    t2, ch = idx // NCH, idx % NCH
                    tg = 2 + t2
                    c.stt(hT[:, ch, t2 * TG:(t2 + 1) * TG], c.XT[:, ch, tg * TG:(tg + 1) * TG],
                          c.gT[:, gidx * 8 + ch:gidx * 8 + ch + 1], rs[t2][:], ALU.mult, ALU.mult)

        for half in range(2):
            if half == 0:
                for t2 in range(2):
                    tg = 2 * half + t2
                    c.norm_to(tg, gidx, lambda ch, t2=t2: hT[:, ch, t2 * TG:(t2 + 1) * TG], c.ps[(c.nbank) % 8], tmp)
                    c.nbank += 1
            for jp in range(NF // 2):
                sg = load2(jp * 256)
                su = load2(DFF + jp * 256)
                if half == 0:
                    for f in (2 * jp, 2 * jp + 1):
                        c.dma_in("pool", WO[:, f, :], [(WO[:, f, :], wl_out[f * 128:(f + 1) * 128, :])], c.wo_sems[f])
                for sub in range(2):
                    j = 2 * jp + sub
                    par = j % 2
                    cs = slice(sub * 128, (sub + 1) * 128)
                    gb = [c.ps[4 * par + 0], c.ps[4 * par + 1]]
                    ub = [c.ps[4 * par + 2], c.ps[4 * par + 3]]
                    for ch in range(NCH):
                        for t2 in range(2):
                            c.mm(gb[t2][:], sg[:, ch, cs], hT[:, ch, t2 * TG:(t2 + 1) * TG], ch == 0, ch == NCH - 1)
                    for ch in range(NCH):
                        for t2 in range(2):
                            c.mm(ub[t2][:], su[:, ch, cs], hT[:, ch, t2 * TG:(t2 + 1) * TG], ch == 0, ch == NCH - 1)
                    for t2 in range(2):
                        c.act(sil[t2][:], gb[t2][:], AF.Silu)
                        c.tt(actT[:, j, t2 * TG:(t2 + 1) * TG], sil[t2][:], ub[t2][:], ALU.mult)
            for m in range(NCH):
                bk = [c.ps[2 * (m % 2)], c.ps[2 * (m % 2) + 1]]
                for f in range(NF):
                    for t2 in range(2):
                        c.mm(bk[t2][:], WO[:, f, m * 128:(m + 1) * 128], actT[:, f, t2 * TG:(t2 + 1) * TG], f == 0, f == NF - 1)
                if half == 0:
                    hoisted_norm_piece(m)
                for t2 in range(2):
                    tg = 2 * half + t2
                    xv = c.XT[:, m, tg * TG:(tg + 1) * TG]
                    c.stt(xv, bk[t2][:], 0.5, xv, ALU.mult, ALU.add)
        c.cur = save

    def mixer(self, l):
        c = self
        save = c.cur
        win = c.d["w_in"][l]
        KT = c.alloc([NCH, T], BF16)
        V = c.alloc([16, D], BF16)
        Hs = c.alloc([8, 640], BF16)
        biasC = c.alloc([4, 16, 4], F32)
        base2 = c.cur
        bv = c.alloc([D], F32)
        c.dma_in("sp", bv[:], [(bv[:], c.d["bv"][l:l + 1, :].partition_broadcast(128))], c.bv_sem)
        bfx = c.alloc([16, 4], F32)
        nlf = c.alloc([16, 4], F32)
        totp = c.alloc([16, 4], F32)
        ncp = c.alloc([16, 4], F32)
        nfc = c.alloc([16, 4], F32)
        nncp = c.alloc([16, 4], F32)
        c.dma_in("sp", bfx[:], [(bfx[:], bass.AP(c.d["bf"].tensor, l * 4, [[0, 128], [0, 16], [1, 4]]))], c.bf_sem)
        hT = c.alloc([NCH, 2 * TG], BF16)
        Wv = c.alloc([NCH, D], BF16)
        Wf = c.alloc([NCH, 4], BF16)
        tmp = c.alloc_norm_tmp()
        srcs = []
        for (vc, n, o) in ((VA, 256, 0), (VB, 512, 256), (VC, 256, 768)):
            srcs.append((Wv[:, :, o:o + n], win[:, vc:vc + n].rearrange("(c p) n -> p c n", p=128)))
        c.dma_in("pool", Wv[:], srcs, c.wv_sem)
        c.dma_in("pool", Wf[:], [(Wf[:], win[:, FOFF:FOFF + 4].rearrange("(c p) n -> p c n", p=128))], c.wf_sem)
        psf = c.ps[7]
        for half in range(2):
            for t2 in range(2):
                tg = 2 * half + t2
                c.norm_to(tg, l * 3 + 1, lambda ch, t2=t2: hT[:, ch, t2 * TG:(t2 + 1) * TG], c.ps[6], tmp)
            for kc in range(8):
                sl = c.load_cols(win, KCOL[kc])
                bk = [c.ps[2 * (kc % 2)], c.ps[2 * (kc % 2) + 1]]
                for ch in range(NCH):
                    for t2 in range(2):
                        c.mm(bk[t2][:], sl[:, ch, :], hT[:, ch, t2 * TG:(t2 + 1) * TG], ch == 0, ch == NCH - 1)
                for t2 in range(2):
                    tg = 2 * half + t2
                    dst = KT[:, kc, tg * TG:(tg + 1) * TG]
                    bcol = c.bfm[:, l, 8 + kc:8 + kc + 1]
                    if t2 == 0:
                        c.ts(dst, bk[t2][:], bcol, None, ALU.add)
                    else:
                        c.act(dst, bk[t2][:], AF.Identity, bias=bcol)
            for blk in range(8):
                ab = 8 * half + blk
                for cg in range(2):
                    bank = c.ps[4 + (2 * blk + cg) % 2]
                    for ch in range(NCH):
                        c.mm(bank[:], hT[:, ch, blk * 128:(blk + 1) * 128], Wv[:, ch, cg * 512:(cg + 1) * 512], ch == 0, ch == NCH - 1)
                    c.tt(V[:, ab, cg * 512:(cg + 1) * 512], bank[:], bv[:, cg * 512:(cg + 1) * 512], ALU.add)
                for ch in range(NCH):
                    c.mm(psf[:, ab * 4:ab * 4 + 4], hT[:, ch, blk * 128:(blk + 1) * 128], Wf[:, ch, :], ch == 0, ch == NCH - 1)
        flat = lambda t: View(t.ap.rearrange("p a b -> p (a b)"), t[:].reg)
        c.tt(flat(totp), psf[:, 0:64], flat(bfx), ALU.add)
        c.act(flat(nfc), flat(totp), AF.Exp, scale=-1.0)
        c.ts(flat(totp), flat(nfc), 1.0, None, ALU.add)
        c.act(flat(nlf), flat(totp), AF.Ln)
        pt = c.ps[6]
        c.mm(pt[:, 64:128], c.triu_f[:], flat(nlf), True, True)
        for b in range(15):
            n = 15 - b
            outv = View(pt.ap[:, (b + 1) * 4:64].rearrange("p (a b) -> p a b", b=4), pt[:, (b + 1) * 4:64].reg)
            rhsv = View(nlf.ap[:, b:b + 1, :].broadcast_to([128, n, 4]), nlf[:, b, :].reg)
            c.mm(outv, c.ones_f[:], rhsv, b == 0, b == 14)
        c.memset(ncp[:, 0, :], 0.0)
        c.memset(nncp[:, 0, :], 0.0)
        pt3 = pt.ap[:, 4:64].rearrange("p (a b) -> p a b", b=4)
        c.S.add("act", lambda e: e.activation(out=ncp.ap[:, 1:16, :], in_=pt3, func=AF.Identity),
                reads=[pt[:, 4:64].reg], writes=[ncp[:, 1:16, :].reg])
        c.S.add("act", lambda e: e.activation(out=nncp.ap[:, 1:16, :], in_=pt3, func=AF.Identity, scale=-1.0),
                reads=[pt[:, 4:64].reg], writes=[nncp[:, 1:16, :].reg])
        c.tt(flat(nfc), pt[:, 64:128], flat(ncp), ALU.add)
        for g in range(NTG):
            na = 4 * g + 4
            for h in range(4):
                c.act(biasC[:, g, 0:na, h], nfc[:, 0:na, h], AF.Identity, bias=nncp[:, 4 * g, h:h + 1])
        c.dbg_dump("KT%d" % l, KT[:], [128, NCH * T], BF16)
        c.dbg_dump("V%d" % l, V[:], [128, 16 * D], BF16)
        c.dbg_dump("biasC%d" % l, biasC[:], [128, 256], F32)
        c.cur = base2
        c.dma_in("pool", Hs[:], [(Hs[:], bass.AP(c.d["relx"].tensor, l * 8 * EXT + 1, [[1, 128], [EXT, 8], [1, 640]]))], c.hs_sem)
        c.memset(Hs[64:128, :, 576:640], NEG)
        c.memset(Hs[0:64, :, 0:64], NEG)
        hTg = c.alloc([NCH, TG], BF16)
        qT = c.alloc([NCH, 2, TG], BF16)
        merged = c.arena_tile(qT.base, [NCH, TG], BF16)
        oT = c.alloc([NCH, TG], BF16)
        base3 = c.cur
        for g in range(NTG):
            c.cur = base3
            if g == 0:
                tmp = c.alloc_norm_tmp()
                c.norm_to(g, l * 3 + 1, lambda ch: hTg[:, ch, :], c.ps[7], tmp)
            c.memset(qT[64:128, :, 0, :], 0.0)
            c.memset(qT[0:64, :, 1, :], 0.0)
            for qc in range(8):
                sl = c.load_cols(win, QCOL[qc])
                bank = c.ps[qc % 2]
                for ch in range(NCH):
                    c.mm(bank[:], sl[:, ch, :], hTg[:, ch, :], ch == 0, ch == NCH - 1)
                c.ts(qT[0:64, qc, 0, :], bank[0:64, :], c.bfm[0:64, l, qc:qc + 1], 0.125, ALU.add, ALU.mult)
                c.ts(qT[64:128, qc, 1, :], bank[64:128, :], c.bfm[64:128, l, qc:qc + 1], 0.125, ALU.add, ALU.mult)
            c.cur = base3
            c.attn_A(g, KT, V, qT, oT)
            c.cur = base3
            c.attn_BC(g, KT, V, qT, oT, Hs, biasC)
            if g == 0:
                pass
                c.dbg_dump("oT%d" % l, oT[:], [128, NCH * TG], BF16)
            c.cur = base3
            c.merge_out(l, g, hTg, merged, oT, qT.base + 8192, (g + 1) if g + 1 < NTG else None)
        c.cur = save

    def attn_A(self, g, KT, V, qT, oT):
        c = self
        e_sb = c.alloc([TG], F32)
        sp_b = [c.alloc([TG], BF16) for _ in range(2)]
        R = c.alloc([TG], F32)
        Rb = [c.alloc([TG], BF16) for _ in range(3)]
        w_b = [c.alloc([TG], BF16) for _ in range(2)]
        steps = []
        for h in range(4):
            for a in reversed(range(4 * g + 4)):
                steps.append((h, a))
        q0 = 4 * g
        n = len(steps)

        def geom(h, a):
            col0 = max(0, a - q0) * 128
            hh = h % 2
            pr = slice(64 * hh, 64 * hh + 64)
            kc = h // 2
            return col0, pr, kc

        def zmm(bank, h, a, stop):
            col0, pr, kc = geom(h, a)
            diag = a >= q0
            c.mm(bank[:, col0:TG], KT[:, kc, a * 128:(a + 1) * 128], qT[:, kc, h % 2, col0:TG], True, stop and not diag)
            if diag:
                c.mm(bank[:, col0:col0 + 128], c.ident_b[:], c.maskA_b[:], False, stop)

        def s1(k):
            h, a = steps[k]
            col0, pr, kc = geom(h, a)
            first = a == 4 * g + 3
            last = a == 0
            z1 = c.ps[k % 2]
            zmm(z1, h, a, True)
            c.act(e_sb[:, col0:TG], z1[:, col0:TG], AF.Exp)
            c.act(sp_b[k % 2][:, col0:TG], e_sb[:, col0:TG], AF.Ln, bias=1.0)
            if not last:
                r = R
                if first:
                    c.memset(r[:], 0.0)
                c.tt(r[:, col0:TG], r[:, col0:TG], sp_b[k % 2][:, col0:TG], ALU.add)
                c.copy(Rb[(k + 1) % 3][:, col0:TG], r[:, col0:TG])

        def s2(k):
            h, a = steps[k]
            col0, pr, kc = geom(h, a)
            first = a == 4 * g + 3
            z2 = c.ps[2 + k % 2]
            zmm(z2, h, a, False)
            colR = max(0, a + 1 - q0) * 128
            hasR = (not first) and colR < TG
            c.mm(z2[:, col0:TG], c.negU_b[:], sp_b[k % 2][:, col0:TG], False, not hasR)
            if hasR:
                c.mm(z2[:, colR:TG], c.negones_b[:], Rb[k % 3][:, colR:TG], False, True)
            c.act(w_b[k % 2][:, col0:TG], z2[:, col0:TG], AF.Exp)

        def s3(k):
            h, a = steps[k]
            col0, pr, kc = geom(h, a)
            first = a == 4 * g + 3
            last = a == 0
            ob = c.ps[4 + 2 * (h % 2)]
            if first:
                c.mm(ob[:, :], c.zeros_b[:, 0:128], c.zeros_b[:], True, False)
            c.mm(ob[:, col0:TG], V[:, a, kc * 128:(kc + 1) * 128], w_b[k % 2][:, col0:TG], False, last)
            if last:
                c.copy(oT[pr, kc, :], ob[pr, :], "dve")

        for k in range(n + 2):
            if k < n:
                s1(k)
            if 0 <= k - 1 < n:
                s2(k - 1)
            if 0 <= k - 2 < n:
                s3(k - 2)

    def attn_BC(self, g, KT, V, qT, oT, Hs, biasC):
        c = self
        p_b = [c.alloc([TG], BF16) for _ in range(3)]
        rden = c.alloc([TG], F32)
        q0 = 4 * g
        steps = []
        for h in range(8):
            first = q0 - 1 if g > 0 else 0
            alist = [first] + [a for a in range(max(0, q0 - 4), q0 + 4) if a != first]
            for i, a in enumerate(alist):
                u0 = 128 * (q0 - a)
                c0 = max(0, -u0)
                c1 = min(TG, 640 - u0)
                steps.append(dict(kind="B", h=h, a=a, kc=2 + h // 2, hv=4 + h, c0=c0, c1=c1, u0=u0,
                                  first=(i == 0), last=(i == len(alist) - 1)))
        for h in range(4):
            na = q0 + 4
            for a in range(na):
                col0 = max(0, a - q0) * 128
                steps.append(dict(kind="C", h=h, a=a, kc=6 + h // 2, hv=12 + h, c0=col0, c1=TG,
                                  first=(a == 0), last=(a == na - 1)))
        n = len(steps)

        def s1(k):
            s = steps[k]
            hh = s["h"] % 2
            pr = slice(64 * hh, 64 * hh + 64)
            a, kc, c0, c1 = s["a"], s["kc"], s["c0"], s["c1"]
            z = c.ps[k % 4]
            c.mm(z[:, c0:c1], KT[:, kc, a * 128:(a + 1) * 128], qT[:, kc, hh, c0:c1], True,
                 s["kind"] == "C" and a < q0)
            if s["kind"] == "B":
                u0 = s["u0"]
                c.mm(z[:, c0:c1], c.J_b[:], Hs[:, s["h"], u0 + c0:u0 + c1], False, True)
                c.act(p_b[k % 3][:, c0:c1], z[:, c0:c1], AF.Exp)
            else:
                if a >= q0:
                    c.mm(z[:, c0:c0 + 128], c.ident_b[:], c.maskC_b[:], False, True)
                c.act(p_b[k % 3][:, c0:c1], z[:, c0:c1], AF.Exp, bias=biasC[:, g, a, s["h"]:s["h"] + 1])

        def s2(k):
            s = steps[k]
            hh = s["h"] % 2
            pr = slice(64 * hh, 64 * hh + 64)
            a, kc, c0, c1, hv = s["a"], s["kc"], s["c0"], s["c1"], s["hv"]
            ob = c.ps[4 + 2 * hh]
            db = c.ps[5 + 2 * hh]
            vc = (hv // 2) * 128
            c.mm(ob[:, c0:c1], V[:, a, vc:vc + 128], p_b[k % 3][:, c0:c1], s["first"], s["last"])
            c.mm(db[:, c0:c1], c.ones_b[:], p_b[k % 3][:, c0:c1], s["first"], s["last"])
            if s["last"]:
                c.act(rden[pr, :], db[pr, :], AF.Ln)
                c.act(rden[pr, :], rden[pr, :], AF.Exp, scale=-1.0)
                c.tt(oT[pr, kc, :], ob[pr, :], rden[pr, :], ALU.mult)

        for k in range(n + 2):
            if k < n:
                s1(k)
            if 0 <= k - 2 < n:
                s2(k - 2)

    def merge_out(self, l, g, hTg, merged, oT, xbase, next_g=None):
        c = self
        win = c.d["w_in"][l]
        ring = list(zip(c.slots, c.slot_sems)) + [(c.arena_tile(xbase + 2048 * i, [NCH, 128], BF16), c.mslot_sems[i]) for i in range(4)]
        st = dict(ri=0)

        def nxt():
            b = ring[st["ri"] % len(ring)]
            st["ri"] += 1
            return b

        def load_cols(wl, col0):
            sl, sem = nxt()
            c.dma_in("pool", sl[:], [(sl[:], wl[:, col0:col0 + 128].rearrange("(c p) n -> p c n", p=128))], sem)
            return sl
        sig = [c.alloc([TG], F32) for _ in range(2)]
        macc = c.alloc([TG], F32)
        tmpm = c.alloc([TG], F32)
        if next_g is not None:
            nsq, nln, nrs = c.alloc_norm_tmp()
        brs = [(0, [0, 1], c.d["w_br_sb"][l]), (1, [2, 3, 4, 5], c.d["w_br_ch"][l]), (2, [6, 7], c.d["w_br_fox"][l])]
        k = 0
        for m in range(NCH):
            sl, sem = nxt()
            srcs = []
            for (bi, chunks, wd) in brs:
                n = len(chunks)
                srcs.append((sl[:, chunks[0]:chunks[0] + n, :], wd[:, m * 128:(m + 1) * 128].rearrange("(c p) n -> p c n", p=128)))
            c.dma_in("pool", sl[:], srcs, sem)
            for (bi, chunks, wd) in brs:
                sg = load_cols(win, GCOL[bi] + m * 128)
                yb = c.ps[bi]
                gb = c.ps[3 + bi]
                for i, ch in enumerate(chunks):
                    c.mm(yb[:], sl[:, ch, :], oT[:, ch, :], i == 0, i == len(chunks) - 1)
                for ch in range(NCH):
                    c.mm(gb[:], sg[:, ch, :], hTg[:, ch, :], ch == 0, ch == NCH - 1)
                s = sig[k % 2]
                k += 1
                c.act(s[:], gb[:], AF.Sigmoid, bias=c.bfm[:, l, 16 + bi * 8 + m:16 + bi * 8 + m + 1])
                if bi == 0:
                    c.tt(macc[:], yb[:], s[:], ALU.mult)
                elif bi == 1:
                    c.tt(tmpm[:], yb[:], s[:], ALU.mult)
                    c.tt(macc[:], macc[:], tmpm[:], ALU.add)
                else:
                    c.tt(tmpm[:], yb[:], s[:], ALU.mult)
                    c.tt(merged[:, m, :], macc[:], tmpm[:], ALU.add)
            if next_g is not None:
                sqv = nsq[m % 2]
                c.act(sqv[:], c.XT[:, m, next_g * TG:(next_g + 1) * TG], AF.Square)
                c.mm(c.ps[7][:], c.ones_b[:], sqv[:], m == 0, m == NCH - 1)
        if next_g is not None:
            c.act(nln[:], c.ps[7][:], AF.Ln, bias=c.eps[:, 0:1], scale=1.0 / D)
            c.act(nrs[:], nln[:], AF.Exp, scale=-0.5)
        wout = c.d["w_out"][l]
        for m in range(NCH):
            sl = load_cols(wout, m * 128)
            bank = c.ps[6 + m % 2]
            for ch in range(NCH):
                c.mm(bank[:], sl[:, ch, :], merged[:, ch, :], ch == 0, ch == NCH - 1)
            if next_g is not None:
                c.stt(hTg[:, m, :], c.XT[:, m, next_g * TG:(next_g + 1) * TG],
                      c.gT[:, (l * 3 + 1) * 8 + m:(l * 3 + 1) * 8 + m + 1], nrs[:], ALU.mult, ALU.mult)
            xv = c.XT[:, m, g * TG:(g + 1) * TG]
            c.tt(xv, bank[:], xv, ALU.add)

    def final(self, do_norm=True):
        c = self
        save = c.cur
        tmp = c.alloc_norm_tmp()
        yT = [c.alloc([TG], F32) for _ in range(NCH)]
        os_ = [c.alloc([2, D], F32) for _ in range(2)]
        sems = [c.newsem("os0"), c.newsem("os1")]
        c.out_sems = sems
        k = 0
        for tg in range(NTG):
            if do_norm:
                rstd = c.norm_stats(tg, c.ps[4 * (k % 2)], tmp)
            for ch in range(NCH):
                xv = c.XT[:, ch, tg * TG:(tg + 1) * TG]
                if do_norm:
                    c.stt(yT[ch][:], xv, c.gT[:, 48 + ch:48 + ch + 1], rstd[:], ALU.mult, ALU.mult)
                else:
                    c.copy(yT[ch][:], xv)
            for hb in range(2):
                par = k % 2
                k += 1
                banks = [c.ps[4 * par + i] for i in range(4)]
                for ch in range(NCH):
                    for bb in range(2):
                        b = 2 * hb + bb
                        bank = banks[2 * bb + ch // 4]
                        c.tr(bank[:, (ch % 4) * 128:(ch % 4 + 1) * 128], yT[ch][:, b * 128:(b + 1) * 128], c.ident_f[:])
                st = os_[par]
                for bb in range(2):
                    for hf in range(2):
                        eng = "dve" if (bb + hf) % 2 == 0 else "act"
                        c.copy(st[:, bb, hf * 512:(hf + 1) * 512], banks[2 * bb + hf][:], eng)
                r0 = tg * TG + hb * 256
                c.dma_out("sp", c.d["out"][r0:r0 + 256, :].rearrange("(b p) d -> p b d", p=128), st[:], sems[par])
        c.cur = save

    def dbg_dump(self, name, view, shape, dtype):
        if name not in self.dbg:
            return
        c = self
        t = c.nc.dram_tensor("dbg_" + name, shape, dtype, kind="ExternalOutput").ap()
        sem = c.newsem("dbg_" + name)
        c.out_sems_extra.append(sem)
        nd = len(view.ap.shape)
        src = view.ap
        if nd == 3:
            src = src.rearrange("p a b -> p (a b)")
        elif nd == 4:
            src = src.rearrange("p a b c -> p (a b c)")
        c.S.add("sp", lambda e: [e.dma_start(out=t, in_=src)], reads=[view.reg], dma=sem, ndma=1)
        c.dbg_out[name] = "dbg_" + name

    def build(self):
        nc = bass.Bass("TRN2", target_bir_lowering=False)
        self.nc = nc
        d = {}

        def din(name, shape):
            d[name] = nc.dram_tensor(name, shape, F32, kind="ExternalInput").ap()

        din("x", [T, D])
        din("w_ffn1_in", [2, D, 2 * DFF])
        din("w_ffn1_out", [2, DFF, D])
        din("w_in", [2, D, INW])
        din("w_br_sb", [2, 256, D])
        din("w_br_ch", [2, 512, D])
        din("w_br_fox", [2, 256, D])
        din("w_out", [2, D, D])
        din("w_ffn2_in", [2, D, 2 * DFF])
        din("w_ffn2_out", [2, DFF, D])
        din("gT", [128, 56])
        din("bfm", [2, 128, 40])
        din("bv", [2, D])
        din("bf", [2, 4])
        din("relx", [2, 8, EXT])
        d["out"] = nc.dram_tensor("out", [T, D], F32, kind="ExternalOutput").ap()
        self.d = d
        self.out_sems_extra = []

        self.arena_bytes = 212800
        with contextlib.ExitStack() as es:
            self.arena = es.enter_context(nc.sbuf_tensor("arena", [128, self.arena_bytes // 2], BF16))
            pst = [es.enter_context(nc.psum_tensor("ps%d" % i, [128, 512], F32)) for i in range(8)]
            self.ps = [Tile(pst[i], "ps", i * 2048, [512], 4) for i in range(8)]
            self.cur = 0
            self.peak = 0
            self.nbank = 0
            self.XT = self.alloc([NCH, T], F32)
            self.nslots = 8
            self.slots = [self.alloc([NCH, 128], BF16) for _ in range(self.nslots)]
            self.slot_sems = [self.newsem("slot%d" % i) for i in range(self.nslots)]
            self.slot_i = 0
            self.wo_sems = [self.newsem("wo%d" % i) for i in range(NF)]
            self.mslot_sems = [self.newsem("mslot%d" % i) for i in range(4)]
            self.bv_sem = self.newsem("bv")
            self.bf_sem = self.newsem("bf")
            self.wv_sem = self.newsem("wv")
            self.wf_sem = self.newsem("wf")
            self.hs_sem = self.newsem("hs")
            self.setup_consts()
            self.load_x()
            stop = self.stop_after
            done = False
            for l in range(self.n_layers):
                self.ffn(l, d["w_ffn1_in"], d["w_ffn1_out"], l * 3 + 0)
                if stop == ("ffn1", l):
                    done = True
                    break
                self.mixer(l)
                if stop == ("mix", l):
                    done = True
                    break
                self.ffn(l, d["w_ffn2_in"], d["w_ffn2_out"], l * 3 + 2)
            self.final(do_norm=not done)
            self.S.finalize()

            sems = {}
            for e in ("pe", "act", "dve", "pool"):
                sems["eng:" + e] = es.enter_context(nc.semaphore("s_" + e))
            for name in self.dma_sems:
                sems["dma:" + name] = es.enter_context(nc.semaphore("d_" + name))
            S = self.S
            final_waits = [(sems["dma:" + s], 16 * S.dma_count[s]) for s in list(self.out_sems) + self.out_sems_extra]
            block = es.enter_context(nc.Block())

            @block.tensor
            def _(e):
                S.emit_engine("pe", e, sems)

            @block.scalar
            def _(e):
                S.emit_engine("act", e, sems)

            @block.vector
            def _(e):
                S.emit_engine("dve", e, sems)

            @block.gpsimd
            def _(e):
                S.emit_engine("pool", e, sems)

            @block.sync
            def _(e):
                S.emit_engine("sp", e, sems)
                for sem, val in final_waits:
                    e.wait_ge(sem, val)
        return nc


def prep_inputs(inputs):
    f = lambda a: np.ascontiguousarray(np.asarray(a, dtype=np.float32))
    g1, gm, g2, gf = f(inputs["g_ffn1"]), f(inputs["g_mix"]), f(inputs["g_ffn2"]), f(inputs["g_final"])
    cols = []
    for l in range(2):
        for gvec in (g1[l], gm[l], g2[l]):
            cols.append(gvec.reshape(8, 128).T)
    cols.append(gf.reshape(8, 128).T)
    gT = f(np.concatenate(cols, axis=1))
    b_in = f(inputs["b_in"])
    bfm = np.zeros((2, 128, 40), np.float32)
    for l in range(2):
        for i, c0 in enumerate(QCOL):
            bfm[l, :, i] = b_in[l, c0:c0 + 128]
        for i, c0 in enumerate(KCOL):
            bfm[l, :, 8 + i] = b_in[l, c0:c0 + 128]
        for bi in range(3):
            for m in range(8):
                c0 = GCOL[bi] + m * 128
                bfm[l, :, 16 + bi * 8 + m] = b_in[l, c0:c0 + 128]
    bv = f(np.concatenate([b_in[:, VA:VA + 256], b_in[:, VB:VB + 512], b_in[:, VC:VC + 256]], axis=1))
    bf = f(b_in[:, FOFF:FOFF + 4])
    rel = f(inputs["rel_bias"])
    idx = np.minimum(np.arange(EXT), 256)
    relx = f(np.transpose(rel, (0, 2, 1))[:, :, idx])
    shared = {
        "gT": gT, "bfm": bfm, "bv": bv, "bf": bf, "relx": relx,
    }
    for k in ("w_ffn1_in", "w_ffn1_out", "w_in", "w_br_sb", "w_br_ch", "w_br_fox", "w_out", "w_ffn2_in", "w_ffn2_out"):
        shared[k] = f(inputs[k])
    return shared


_NC_CACHE = {}


def kernel(**inputs):
    x = np.ascontiguousarray(np.asarray(inputs["x"], dtype=np.float32))
    shared = prep_inputs(inputs)
    if "nc" not in _NC_CACHE:
        _NC_CACHE["nc"] = Builder().build()
    nc = _NC_CACHE["nc"]
    in_maps = []
    for b in range(8):
        m = dict(shared)
        m["x"] = x[b]
        in_maps.append(m)
    res = run_bass_kernel_spmd(nc, in_maps, core_ids=list(range(8)))
    return np.stack([np.asarray(r["out"], dtype=np.float32) for r in res.results], axis=0)
```

```python
import contextlib
from collections import defaultdict

import numpy as np
import concourse.bass as bass
import concourse.mybir as mybir
from concourse.bass_utils import run_bass_kernel_spmd

F32 = mybir.dt.float32
BF16 = mybir.dt.bfloat16
AF = mybir.ActivationFunctionType
ALU = mybir.AluOpType

D = 1024
NCH = 8
T = 2048
TG = 512
NTG = 4
DFF = 2816
NF = 22
INW = 6148
QCOL = [0, 128, 768, 896, 1024, 1152, 2304, 2432]
KCOL = [256, 384, 1280, 1408, 1536, 1664, 2560, 2688]
VA, VB, VC = 512, 1792, 2816
FOFF = 3072
GCOL = [3076, 4100, 5124]
NEG = -30000.0
EXT = 768


class View:
    __slots__ = ("ap", "reg")

    def __init__(self, ap, reg):
        self.ap = ap
        self.reg = reg


class Tile:
    def __init__(self, ap, space, base, shape, esize):
        self.ap = ap
        self.space = space
        self.base = base
        self.shape = tuple(shape)
        self.esize = esize
        st = []
        s = 1
        for d in reversed(self.shape):
            st.append(s)
            s *= d
        self.strides = tuple(reversed(st))
        self.nbytes = s * esize

    def __getitem__(self, idx):
        if not isinstance(idx, tuple):
            idx = (idx,)
        idx = idx + (slice(None),) * (1 + len(self.shape) - len(idx))
        p = idx[0]
        p0, p1, _ = p.indices(128)
        lo = hi = 0
        for dim, stv, ix in zip(self.shape, self.strides, idx[1:]):
            if isinstance(ix, int):
                lo += ix * stv
                hi += ix * stv
            else:
                a, b, _ = ix.indices(dim)
                lo += a * stv
                hi += (b - 1) * stv
        b0 = self.base + lo * self.esize
        b1 = self.base + (hi + 1) * self.esize
        return View(self.ap[idx], (self.space, p0, p1, b0, b1))


class Sched:
    PAGE = 1024

    def __init__(self):
        self.ops = []
        self.eng_ops = defaultdict(list)
        self.pages = defaultdict(set)
        self.recs = {}
        self.nrec = 0
        self.dma_count = defaultdict(int)

    def _cands(self, reg):
        sp, p0, p1, b0, b1 = reg
        out = set()
        for pg in range(b0 // self.PAGE, (b1 - 1) // self.PAGE + 1):
            out |= self.pages.get((sp, pg), set())
        res = []
        for rid in out:
            r = self.recs[rid]
            _, q0, q1, c0, c1 = r[0]
            if q0 < p1 and p0 < q1 and c0 < b1 and b0 < c1:
                res.append(rid)
        return res

    def _newrec(self, reg, writer, readers):
        rid = self.nrec
        self.nrec += 1
        self.recs[rid] = [reg, writer, readers]
        sp, p0, p1, b0, b1 = reg
        for pg in range(b0 // self.PAGE, (b1 - 1) // self.PAGE + 1):
            self.pages[(sp, pg)].add(rid)
        return rid

    def _delrec(self, rid):
        reg = self.recs.pop(rid)[0]
        sp, p0, p1, b0, b1 = reg
        for pg in range(b0 // self.PAGE, (b1 - 1) // self.PAGE + 1):
            self.pages[(sp, pg)].discard(rid)

    @staticmethod
    def _ovl(a, b):
        return a[1] < b[2] and b[1] < a[2] and a[3] < b[4] and b[3] < a[4]

    @staticmethod
    def _contains(a, b):
        return a[1] <= b[1] and b[2] <= a[2] and a[3] <= b[3] and b[4] <= a[4]

    @staticmethod
    def _remainder(r, w):
        sp, p0, p1, b0, b1 = r
        _, q0, q1, c0, c1 = w
        out = []
        lo = max(b0, c0)
        hi = min(b1, c1)
        if b0 < lo:
            out.append((sp, p0, p1, b0, lo))
        if hi < b1:
            out.append((sp, p0, p1, hi, b1))
        if p0 < q0:
            out.append((sp, p0, min(p1, q0), lo, hi))
        if q1 < p1:
            out.append((sp, max(p0, q1), p1, lo, hi))
        return out

    def add(self, eng, fn, reads=(), writes=(), dma=None, ndma=1):
        gid = len(self.ops)
        deps = set()
        for reg in reads:
            covered = False
            for rid in self._cands(reg):
                r = self.recs[rid]
                if r[1] is not None:
                    deps.add(r[1])
                r[2].append((gid, reg))
                if self._contains(r[0], reg):
                    covered = True
            if not covered:
                self._newrec(reg, None, [(gid, reg)])
        for reg in writes:
            for rid in self._cands(reg):
                r = self.recs[rid]
                if r[1] is not None:
                    deps.add(r[1])
                for (g2, rr) in r[2]:
                    if self._ovl(rr, reg):
                        deps.add(g2)
                pieces = self._remainder(r[0], reg)
                self._delrec(rid)
                for pc in pieces:
                    self._newrec(pc, r[1], [(g2, rr) for (g2, rr) in r[2] if self._ovl(rr, pc)])
            self._newrec(reg, gid, [])
        deps.discard(gid)
        op = dict(eng=eng, fn=fn, deps=deps, dma=dma, ndma=ndma, signal=False, local=len(self.eng_ops[eng]))
        if dma is not None:
            self.dma_count[dma] += ndma
            op["dmaval"] = 16 * self.dma_count[dma]
        if eng == "pe":
            op["deps"] = {d for d in deps if not (self.ops[d]["eng"] == "pe" and self.ops[d]["dma"] is None)}
        for d in op["deps"]:
            self.ops[d]["signal"] = True
        self.ops.append(op)
        self.eng_ops[eng].append(gid)
        return gid

    def finalize(self):
        for eng, lst in self.eng_ops.items():
            cnt = 0
            for gid in lst:
                op = self.ops[gid]
                if op["dma"] is None and op["signal"]:
                    cnt += 1
                    op["sigval"] = cnt

    def emit_engine(self, eng, handle, sems):
        seen = {}
        for gid in self.eng_ops[eng]:
            op = self.ops[gid]
            need = {}
            for d in op["deps"]:
                pd = self.ops[d]
                if pd["dma"] is not None:
                    key, val = "dma:" + pd["dma"], pd["dmaval"]
                else:
                    key, val = "eng:" + pd["eng"], pd["sigval"]
                if need.get(key, 0) < val:
                    need[key] = val
            for key, val in need.items():
                if seen.get(key, 0) >= val:
                    continue
                handle.wait_ge(sems[key], val)
                seen[key] = val
            insts = op["fn"](handle)
            if op["dma"] is not None:
                if not isinstance(insts, (list, tuple)):
                    insts = [insts]
                assert len(insts) == op["ndma"]
                for ins in insts:
                    ins.then_inc(sems["dma:" + op["dma"]], 16)
            elif op["signal"]:
                if isinstance(insts, (list, tuple)):
                    insts = insts[-1]
                insts.then_inc(sems["eng:" + eng], 1)


class Builder:
    def __init__(self, n_layers=2, dbg=None, stop_after=None):
        self.n_layers = n_layers
        self.dbg = dbg or {}
        self.stop_after = stop_after
        self.S = Sched()
        self.dma_sems = []
        self.out_sems = []
        self.dbg_out = {}

    def arena_tile(self, off, shape, dtype):
        es = 4 if dtype == F32 else 2
        n = 1
        for d in shape:
            n *= d
        nbytes = n * es
        assert off % 4 == 0
        assert off + nbytes <= self.arena_bytes, (off, nbytes, self.arena_bytes)
        ap = self.arena[:, off // 2:(off + nbytes) // 2]
        if dtype == F32:
            ap = ap.bitcast(F32)
        if len(shape) == 2:
            ap = ap.rearrange("p (a b) -> p a b", a=shape[0])
        elif len(shape) == 3:
            ap = ap.rearrange("p (a b c) -> p a b c", a=shape[0], b=shape[1])
        return Tile(ap, "sb", off, shape, es)

    def alloc(self, shape, dtype):
        es = 4 if dtype == F32 else 2
        n = 1
        for d in shape:
            n *= d
        nbytes = (n * es + 31) // 32 * 32
        t = self.arena_tile(self.cur, shape, dtype)
        self.cur += nbytes
        self.peak = max(self.peak, self.cur)
        return t

    def newsem(self, name):
        self.dma_sems.append(name)
        return name

    def mm(self, out, lhsT, rhs, start, stop):
        self.S.add("pe", lambda e: e.matmul(out.ap, lhsT.ap, rhs.ap, start=start, stop=stop),
                   reads=[lhsT.reg, rhs.reg], writes=[out.reg])

    def tr(self, out, in_, ident):
        self.S.add("pe", lambda e: e.transpose(out.ap, in_.ap, ident.ap),
                   reads=[in_.reg, ident.reg], writes=[out.reg])

    def act(self, out, in_, func, bias=None, scale=1.0):
        reads = [in_.reg]
        if isinstance(bias, View):
            reads.append(bias.reg)
            b = bias.ap
        elif bias is None:
            b = 0.0
        else:
            b = bias
        self.S.add("act", lambda e: e.activation(out=out.ap, in_=in_.ap, func=func, bias=b, scale=scale),
                   reads=reads, writes=[out.reg])

    def tt(self, out, in0, in1, op, eng="dve"):
        self.S.add(eng, lambda e: e.tensor_tensor(out=out.ap, in0=in0.ap, in1=in1.ap, op=op),
                   reads=[in0.reg, in1.reg], writes=[out.reg])

    def ts(self, out, in0, s1, s2, op0, op1=None, eng="dve"):
        reads = [in0.reg]
        a1 = s1
        a2 = s2
        if isinstance(s1, View):
            reads.append(s1.reg)
            a1 = s1.ap
        if isinstance(s2, View):
            reads.append(s2.reg)
            a2 = s2.ap
        if op1 is None:
            self.S.add(eng, lambda e: e.tensor_scalar(out=out.ap, in0=in0.ap, scalar1=a1, scalar2=None, op0=op0),
                       reads=reads, writes=[out.reg])
        else:
            self.S.add(eng, lambda e: e.tensor_scalar(out=out.ap, in0=in0.ap, scalar1=a1, scalar2=a2, op0=op0, op1=op1),
                       reads=reads, writes=[out.reg])

    def stt(self, out, in0, scalar, in1, op0, op1):
        reads = [in0.reg, in1.reg]
        a = scalar
        if isinstance(scalar, View):
            reads.append(scalar.reg)
            a = scalar.ap
        self.S.add("dve", lambda e: e.scalar_tensor_tensor(out=out.ap, in0=in0.ap, scalar=a, in1=in1.ap, op0=op0, op1=op1),
                   reads=reads, writes=[out.reg])

    def copy(self, out, in_, eng="dve"):
        if eng == "act":
            self.act(out, in_, AF.Copy)
        else:
            self.S.add(eng, lambda e: e.tensor_copy(out=out.ap, in_=in_.ap), reads=[in_.reg], writes=[out.reg])

    def recip(self, out, in_):
        self.S.add("dve", lambda e: e.reciprocal(out=out.ap, in_=in_.ap), reads=[in_.reg], writes=[out.reg])

    def memset(self, out, val, eng="dve"):
        self.S.add(eng, lambda e: e.memset(out.ap, val), writes=[out.reg])

    def aselect(self, out, in_, pattern, base, cm, cmp, fill):
        self.S.add("pool", lambda e: e.affine_select(out=out.ap, in_=in_.ap, pattern=pattern, base=base,
                                                     channel_multiplier=cm, compare_op=cmp, fill=fill),
                   reads=[in_.reg], writes=[out.reg])

    def dma_in(self, queue, out, srcs, sem):
        def fn(e):
            return [e.dma_start(out=o.ap, in_=s) for o, s in srcs]
        self.S.add(queue, fn, writes=[out.reg], dma=sem, ndma=len(srcs))

    def dma_out(self, queue, dst_ap, src, sem):
        self.S.add(queue, lambda e: [e.dma_start(out=dst_ap, in_=src.ap)], reads=[src.reg], dma=sem, ndma=1)

    def slot(self):
        i = self.slot_i % self.nslots
        self.slot_i += 1
        return self.slots[i], self.slot_sems[i]

    def load_cols(self, wl, col0, ncol=128, nch=NCH):
        sl, sem = self.slot()
        v = sl[:, 0:nch, 0:ncol]
        self.dma_in("pool", v, [(v, wl[:, col0:col0 + ncol].rearrange("(c p) n -> p c n", p=128))], sem)
        return sl

    def setup_consts(self):
        c = self
        c.ident_f = c.alloc([128], F32)
        c.ones_f = c.alloc([128], F32)
        c.triu_f = c.alloc([128], F32)
        c.ident_b = c.alloc([128], BF16)
        c.J_b = c.alloc([128], BF16)
        c.ones_b = c.alloc([128], BF16)
        c.negU_b = c.alloc([128], BF16)
        c.negones_b = c.alloc([128], BF16)
        c.maskA_b = c.alloc([128], BF16)
        c.maskC_b = c.alloc([128], BF16)
        c.zeros_b = c.alloc([512], BF16)
        c.eps = c.alloc([1], F32)
        c.gT = c.alloc([56], F32)
        c.bfm = c.alloc([2, 40], F32)
        pat = [[1, 128]]
        c.memset(c.ones_f[:], 1.0, "pool")
        c.aselect(c.ident_f[:], c.ones_f[:], pat, 0, -1, ALU.is_equal, 0.0)
        c.aselect(c.triu_f[:], c.ones_f[:], pat, 0, -1, ALU.is_ge, 0.0)
        c.memset(c.ones_b[:], 1.0, "pool")
        c.memset(c.negones_b[:], -1.0, "pool")
        c.memset(c.zeros_b[:], 0.0, "pool")
        c.memset(c.eps[:], 1e-6, "pool")
        c.aselect(c.ident_b[:], c.ones_b[:], pat, 0, -1, ALU.is_equal, 0.0)
        c.aselect(c.J_b[:], c.ones_b[:], pat, -127, 1, ALU.is_equal, 0.0)
        c.aselect(c.negU_b[:], c.negones_b[:], [[-1, 128]], 0, 1, ALU.is_ge, 0.0)
        c.aselect(c.maskA_b[:], c.zeros_b[:, 0:128], pat, 0, -1, ALU.is_gt, NEG)
        c.aselect(c.maskC_b[:], c.zeros_b[:, 0:128], pat, 0, -1, ALU.is_ge, NEG)
        sem = c.newsem("vec")
        c.dma_in("sp", c.gT[:], [(c.gT[:], c.d["gT"])], sem)
        sem = c.newsem("bfm")
        c.dma_in("sp", c.bfm[:], [(c.bfm[:], c.d["bfm"].rearrange("l p n -> p l n"))], sem)

    def load_x(self):
        c = self
        save = c.cur
        xs = [c.alloc([4, D], F32) for _ in range(2)]
        sems = [c.newsem("xs0"), c.newsem("xs1")]
        k = 0
        for tg in range(NTG):
            st = xs[tg % 2]
            c.dma_in("sp", st[:], [(st[:], c.d["x"][tg * TG:(tg + 1) * TG, :].rearrange("(b p) d -> p b d", p=128))], sems[tg % 2])
            for ch in range(NCH):
                bank = c.ps[k % 8]
                for b in range(4):
                    c.tr(bank[:, b * 128:(b + 1) * 128], st[:, b, ch * 128:(ch + 1) * 128], c.ident_f[:])
                dst = c.XT[:, ch, tg * TG:(tg + 1) * TG]
                if k % 2 == 0:
                    c.copy(dst, bank[:], "dve")
                else:
                    c.copy(dst, bank[:], "act")
                k += 1
        c.cur = save

    def norm_stats(self, tg, bank, tmp):
        c = self
        sq, lnt, rstd = tmp
        for ch in range(NCH):
            s = sq[ch % 2]
            c.act(s[:], c.XT[:, ch, tg * TG:(tg + 1) * TG], AF.Square)
            c.mm(bank[:], c.ones_b[:], s[:], ch == 0, ch == NCH - 1)
        c.act(lnt[:], bank[:], AF.Ln, bias=c.eps[:, 0:1], scale=1.0 / D)
        c.act(rstd[:], lnt[:], AF.Exp, scale=-0.5)
        return rstd

    def norm_to(self, tg, gidx, dst_fn, bank, tmp):
        c = self
        rstd = c.norm_stats(tg, bank, tmp)
        for ch in range(NCH):
            c.stt(dst_fn(ch), c.XT[:, ch, tg * TG:(tg + 1) * TG], c.gT[:, gidx * 8 + ch:gidx * 8 + ch + 1], rstd[:],
                  ALU.mult, ALU.mult)

    def alloc_norm_tmp(self):
        c = self
        return ([c.alloc([TG], BF16), c.alloc([TG], BF16)], c.alloc([TG], F32), c.alloc([TG], F32))

    def ffn(self, l, w_in_d, w_out_d, gidx, hoist_next_gidx=None, skip_norm0=False):
        c = self
        save = c.cur
        hT = c.alloc([NCH, 2 * TG], BF16)
        actT = c.alloc([NF, 2 * TG], BF16)
        WO = c.alloc([NF, D], BF16)
        tmp = c.alloc_norm_tmp()
        sqx = list(tmp[0]) + [c.alloc([TG], BF16) for _ in range(6)]
        sil = [c.alloc([TG], F32) for _ in range(2)]
        big = [(c.arena_tile(c.slots[2 * i].base, [NCH, 256], BF16), c.slot_sems[2 * i]) for i in range(4)]
        wl_in = w_in_d[l]
        wl_out = w_out_d[l]
        st = dict(ri=0)

        def load2(col0):
            sl, sem = big[st["ri"] % 4]
            st["ri"] += 1
            c.dma_in("pool", sl[:], [(sl[:], wl_in[:, col0:col0 + 256].rearrange("(c p) n -> p c n", p=128))], sem)
            return sl

        def hoisted_norm_piece(m, tbase=2, gi=None):
            gi = gidx if gi is None else gi
            rs = [tmp[1], tmp[2]]
            if m in (0, 1):
                tg = tbase + m
                for ch in range(NCH):
                    c.act(sqx[ch][:], c.XT[:, ch, tg * TG:(tg + 1) * TG], AF.Square)
                    c.mm(c.ps[6 + m][:], c.ones_b[:], sqx[ch][:], ch == 0, ch == NCH - 1)
            elif m == 2:
                for t2 in range(2):
                    c.act(rs[t2][:], c.ps[6 + t2][:], AF.Ln, bias=c.eps[:, 0:1], scale=1.0 / D)
                    c.act(rs[t2][:], rs[t2][:], AF.Exp, scale=-0.5)
            elif m <= 6:
                for i in range(4):
                    idx = (m - 3) * 4 + i
                    t2, ch = idx // NCH, idx % NCH
                    tg = tbase + t2
                    c.stt(hT[:, ch, t2 * TG:(t2 + 1) * TG], c.XT[:, ch, tg * TG:(tg + 1) * TG],
                          c.gT[:, gi * 8 + ch:gi * 8 + ch + 1], rs[t2][:], ALU.mult, ALU.mult)

        for half in range(2):
            if half == 0 and not skip_norm0:
                for t2 in range(2):
                    tg = 2 * half + t2
                    c.norm_to(tg, gidx, lambda ch, t2=t2: hT[:, ch, t2 * TG:(t2 + 1) * TG], c.ps[(c.nbank) % 8], tmp)
                    c.nbank += 1
            for jp in range(NF // 2):
                sg = load2(jp * 256)
                su = load2(DFF + jp * 256)
                if half == 0:
                    for f in (2 * jp, 2 * jp + 1):
                        c.dma_in("pool", WO[:, f, :], [(WO[:, f, :], wl_out[f * 128:(f + 1) * 128, :])], c.wo_sems[f])
                for sub in range(2):
                    j = 2 * jp + sub
                    par = j % 2
                    cs = slice(sub * 128, (sub + 1) * 128)
                    gb = [c.ps[4 * par + 0], c.ps[4 * par + 1]]
                    ub = [c.ps[4 * par + 2], c.ps[4 * par + 3]]
                    for ch in range(NCH):
                        for t2 in range(2):
                            c.mm(gb[t2][:], sg[:, ch, cs], hT[:, ch, t2 * TG:(t2 + 1) * TG], ch == 0, ch == NCH - 1)
                    for ch in range(NCH):
                        for t2 in range(2):
                            c.mm(ub[t2][:], su[:, ch, cs], hT[:, ch, t2 * TG:(t2 + 1) * TG], ch == 0, ch == NCH - 1)
                    for t2 in range(2):
                        c.act(sil[t2][:], gb[t2][:], AF.Silu)
                        c.tt(actT[:, j, t2 * TG:(t2 + 1) * TG], sil[t2][:], ub[t2][:], ALU.mult)
            for m in range(NCH):
                bk = [c.ps[2 * (m % 2)], c.ps[2 * (m % 2) + 1]]
                for f in range(NF):
                    for t2 in range(2):
                        c.mm(bk[t2][:], WO[:, f, m * 128:(m + 1) * 128], actT[:, f, t2 * TG:(t2 + 1) * TG], f == 0, f == NF - 1)
                if half == 0:
                    hoisted_norm_piece(m)
                elif hoist_next_gidx is not None:
                    hoisted_norm_piece(m, 0, hoist_next_gidx)
                for t2 in range(2):
                    tg = 2 * half + t2
                    xv = c.XT[:, m, tg * TG:(tg + 1) * TG]
                    c.stt(xv, bk[t2][:], 0.5, xv, ALU.mult, ALU.add)
        c.cur = save

    def mixer(self, l):
        c = self
        save = c.cur
        win = c.d["w_in"][l]
        KT = c.alloc([NCH, T], BF16)
        V = c.alloc([16, D], BF16)
        Hs = c.alloc([8, 640], BF16)
        biasC = c.alloc([4, 16, 4], F32)
        base2 = c.cur
        bv = c.alloc([D], F32)
        c.dma_in("sp", bv[:], [(bv[:], c.d["bv"][l:l + 1, :].partition_broadcast(128))], c.bv_sem)
        bfx = c.alloc([16, 4], F32)
        nlf = c.alloc([16, 4], F32)
        totp = c.alloc([16, 4], F32)
        ncp = c.alloc([16, 4], F32)
        nfc = c.alloc([16, 4], F32)
        nncp = c.alloc([16, 4], F32)
        c.dma_in("sp", bfx[:], [(bfx[:], bass.AP(c.d["bf"].tensor, l * 4, [[0, 128], [0, 16], [1, 4]]))], c.bf_sem)
        hT = c.alloc([NCH, 2 * TG], BF16)
        Wv = c.alloc([NCH, D], BF16)
        Wf = c.alloc([NCH, 4], BF16)
        tmp = c.alloc_norm_tmp()
        srcs = []
        for (vc, n, o) in ((VA, 256, 0), (VB, 512, 256), (VC, 256, 768)):
            srcs.append((Wv[:, :, o:o + n], win[:, vc:vc + n].rearrange("(c p) n -> p c n", p=128)))
        c.dma_in("pool", Wv[:], srcs, c.wv_sem)
        c.dma_in("pool", Wf[:], [(Wf[:], win[:, FOFF:FOFF + 4].rearrange("(c p) n -> p c n", p=128))], c.wf_sem)
        psf = c.ps[7]
        for half in range(2):
            for t2 in range(2):
                tg = 2 * half + t2
                c.norm_to(tg, l * 3 + 1, lambda ch, t2=t2: hT[:, ch, t2 * TG:(t2 + 1) * TG], c.ps[6], tmp)
            for kc in range(8):
                sl = c.load_cols(win, KCOL[kc])
                bk = [c.ps[2 * (kc % 2)], c.ps[2 * (kc % 2) + 1]]
                for ch in range(NCH):
                    for t2 in range(2):
                        c.mm(bk[t2][:], sl[:, ch, :], hT[:, ch, t2 * TG:(t2 + 1) * TG], ch == 0, ch == NCH - 1)
                for t2 in range(2):
                    tg = 2 * half + t2
                    dst = KT[:, kc, tg * TG:(tg + 1) * TG]
                    bcol = c.bfm[:, l, 8 + kc:8 + kc + 1]
                    if t2 == 0:
                        c.ts(dst, bk[t2][:], bcol, None, ALU.add)
                    else:
                        c.act(dst, bk[t2][:], AF.Identity, bias=bcol)
            for blk in range(8):
                ab = 8 * half + blk
                for cg in range(2):
                    bank = c.ps[4 + (2 * blk + cg) % 2]
                    for ch in range(NCH):
                        c.mm(bank[:], hT[:, ch, blk * 128:(blk + 1) * 128], Wv[:, ch, cg * 512:(cg + 1) * 512], ch == 0, ch == NCH - 1)
                    c.tt(V[:, ab, cg * 512:(cg + 1) * 512], bank[:], bv[:, cg * 512:(cg + 1) * 512], ALU.add)
                for ch in range(NCH):
                    c.mm(psf[:, ab * 4:ab * 4 + 4], hT[:, ch, blk * 128:(blk + 1) * 128], Wf[:, ch, :], ch == 0, ch == NCH - 1)
        flat = lambda t: View(t.ap.rearrange("p a b -> p (a b)"), t[:].reg)
        c.tt(flat(totp), psf[:, 0:64], flat(bfx), ALU.add)
        c.act(flat(nfc), flat(totp), AF.Exp, scale=-1.0)
        c.ts(flat(totp), flat(nfc), 1.0, None, ALU.add)
        c.act(flat(nlf), flat(totp), AF.Ln)
        pt = c.ps[6]
        c.mm(pt[:, 64:128], c.triu_f[:], flat(nlf), True, True)
        for b in range(15):
            n = 15 - b
            outv = View(pt.ap[:, (b + 1) * 4:64].rearrange("p (a b) -> p a b", b=4), pt[:, (b + 1) * 4:64].reg)
            rhsv = View(nlf.ap[:, b:b + 1, :].broadcast_to([128, n, 4]), nlf[:, b, :].reg)
            c.mm(outv, c.ones_f[:], rhsv, b == 0, b == 14)
        c.memset(ncp[:, 0, :], 0.0)
        c.memset(nncp[:, 0, :], 0.0)
        pt3 = pt.ap[:, 4:64].rearrange("p (a b) -> p a b", b=4)
        c.S.add("act", lambda e: e.activation(out=ncp.ap[:, 1:16, :], in_=pt3, func=AF.Identity),
                reads=[pt[:, 4:64].reg], writes=[ncp[:, 1:16, :].reg])
        c.S.add("act", lambda e: e.activation(out=nncp.ap[:, 1:16, :], in_=pt3, func=AF.Identity, scale=-1.0),
                reads=[pt[:, 4:64].reg], writes=[nncp[:, 1:16, :].reg])
        c.tt(flat(nfc), pt[:, 64:128], flat(ncp), ALU.add)
        for g in range(NTG):
            na = 4 * g + 4
            for h in range(4):
                c.act(biasC[:, g, 0:na, h], nfc[:, 0:na, h], AF.Identity, bias=nncp[:, 4 * g, h:h + 1])
        c.dbg_dump("KT%d" % l, KT[:], [128, NCH * T], BF16)
        c.dbg_dump("V%d" % l, V[:], [128, 16 * D], BF16)
        c.dbg_dump("biasC%d" % l, biasC[:], [128, 256], F32)
        c.cur = base2
        c.dma_in("pool", Hs[:], [(Hs[:], bass.AP(c.d["relx"].tensor, l * 8 * EXT + 1, [[1, 128], [EXT, 8], [1, 640]]))], c.hs_sem)
        c.memset(Hs[64:128, :, 576:640], NEG)
        c.memset(Hs[0:64, :, 0:64], NEG)
        hTg = c.alloc([NCH, TG], BF16)
        qT = c.alloc([NCH, 2, TG], BF16)
        merged = c.arena_tile(qT.base, [NCH, TG], BF16)
        oT = c.alloc([NCH, TG], BF16)
        base3 = c.cur
        for g in range(NTG):
            c.cur = base3
            if g == 0:
                tmp = c.alloc_norm_tmp()
                c.norm_to(g, l * 3 + 1, lambda ch: hTg[:, ch, :], c.ps[7], tmp)
            c.memset(qT[64:128, :, 0, :], 0.0)
            c.memset(qT[0:64, :, 1, :], 0.0)
            for qc in range(8):
                sl = c.load_cols(win, QCOL[qc])
                bank = c.ps[qc % 2]
                for ch in range(NCH):
                    c.mm(bank[:], sl[:, ch, :], hTg[:, ch, :], ch == 0, ch == NCH - 1)
                c.ts(qT[0:64, qc, 0, :], bank[0:64, :], c.bfm[0:64, l, qc:qc + 1], 0.125, ALU.add, ALU.mult)
                c.ts(qT[64:128, qc, 1, :], bank[64:128, :], c.bfm[64:128, l, qc:qc + 1], 0.125, ALU.add, ALU.mult)
            c.cur = base3
            c.attn_A(g, KT, V, qT, oT)
            c.cur = base3
            c.attn_BC(g, KT, V, qT, oT, Hs, biasC)
            if g == 0:
                pass
                c.dbg_dump("oT%d" % l, oT[:], [128, NCH * TG], BF16)
            c.cur = base3
            c.merge_out(l, g, hTg, merged, oT, qT.base + 8192, (g + 1) if g + 1 < NTG else None)
        c.cur = save

    def attn_A(self, g, KT, V, qT, oT):
        c = self
        e_sb = c.alloc([TG], F32)
        sp_b = [c.alloc([TG], BF16) for _ in range(2)]
        R = c.alloc([TG], F32)
        Rb = [c.alloc([TG], BF16) for _ in range(3)]
        w_b = [c.alloc([TG], BF16) for _ in range(2)]
        steps = []
        for h in range(4):
            for a in reversed(range(4 * g + 4)):
                steps.append((h, a))
        q0 = 4 * g
        n = len(steps)

        def geom(h, a):
            col0 = max(0, a - q0) * 128
            hh = h % 2
            pr = slice(64 * hh, 64 * hh + 64)
            kc = h // 2
            return col0, pr, kc

        def zmm(bank, h, a, stop):
            col0, pr, kc = geom(h, a)
            diag = a >= q0
            c.mm(bank[:, col0:TG], KT[:, kc, a * 128:(a + 1) * 128], qT[:, kc, h % 2, col0:TG], True, stop and not diag)
            if diag:
                c.mm(bank[:, col0:col0 + 128], c.ident_b[:], c.maskA_b[:], False, stop)

        def s1(k):
            h, a = steps[k]
            col0, pr, kc = geom(h, a)
            first = a == 4 * g + 3
            last = a == 0
            z1 = c.ps[k % 2]
            zmm(z1, h, a, True)
            c.act(e_sb[:, col0:TG], z1[:, col0:TG], AF.Exp)
            c.act(sp_b[k % 2][:, col0:TG], e_sb[:, col0:TG], AF.Ln, bias=1.0)
            if not last:
                r = R
                if first:
                    c.memset(r[:], 0.0)
                c.tt(r[:, col0:TG], r[:, col0:TG], sp_b[k % 2][:, col0:TG], ALU.add)
                c.copy(Rb[(k + 1) % 3][:, col0:TG], r[:, col0:TG])

        def s2(k):
            h, a = steps[k]
            col0, pr, kc = geom(h, a)
            first = a == 4 * g + 3
            z2 = c.ps[2 + k % 2]
            zmm(z2, h, a, False)
            colR = max(0, a + 1 - q0) * 128
            hasR = (not first) and colR < TG
            c.mm(z2[:, col0:TG], c.negU_b[:], sp_b[k % 2][:, col0:TG], False, not hasR)
            if hasR:
                c.mm(z2[:, colR:TG], c.negones_b[:], Rb[k % 3][:, colR:TG], False, True)
            c.act(w_b[k % 2][:, col0:TG], z2[:, col0:TG], AF.Exp)

        def s3(k):
            h, a = steps[k]
            col0, pr, kc = geom(h, a)
            first = a == 4 * g + 3
            last = a == 0
            ob = c.ps[4 + 2 * (h % 2)]
            if first:
                c.mm(ob[:, :], c.zeros_b[:, 0:128], c.zeros_b[:], True, False)
            c.mm(ob[:, col0:TG], V[:, a, kc * 128:(kc + 1) * 128], w_b[k % 2][:, col0:TG], False, last)
            if last:
                c.copy(oT[pr, kc, :], ob[pr, :], "dve")

        for k in range(n + 2):
            if k < n:
                s1(k)
            if 0 <= k - 1 < n:
                s2(k - 1)
            if 0 <= k - 2 < n:
                s3(k - 2)

    def attn_BC(self, g, KT, V, qT, oT, Hs, biasC):
        c = self
        p_b = [c.alloc([TG], BF16) for _ in range(3)]
        rden = c.alloc([TG], F32)
        q0 = 4 * g
        steps = []
        for h in range(8):
            first = q0 - 1 if g > 0 else 0
            alist = [first] + [a for a in range(max(0, q0 - 4), q0 + 4) if a != first]
            for i, a in enumerate(alist):
                u0 = 128 * (q0 - a)
                c0 = max(0, -u0)
                c1 = min(TG, 640 - u0)
                steps.append(dict(kind="B", h=h, a=a, kc=2 + h // 2, hv=4 + h, c0=c0, c1=c1, u0=u0,
                                  first=(i == 0), last=(i == len(alist) - 1)))
        for h in range(4):
            na = q0 + 4
            for a in range(na):
                col0 = max(0, a - q0) * 128
                steps.append(dict(kind="C", h=h, a=a, kc=6 + h // 2, hv=12 + h, c0=col0, c1=TG,
                                  first=(a == 0), last=(a == na - 1)))
        n = len(steps)

        def s1(k):
            s = steps[k]
            hh = s["h"] % 2
            pr = slice(64 * hh, 64 * hh + 64)
            a, kc, c0, c1 = s["a"], s["kc"], s["c0"], s["c1"]
            z = c.ps[k % 4]
            c.mm(z[:, c0:c1], KT[:, kc, a * 128:(a + 1) * 128], qT[:, kc, hh, c0:c1], True,
                 s["kind"] == "C" and a < q0)
            if s["kind"] == "B":
                u0 = s["u0"]
                c.mm(z[:, c0:c1], c.J_b[:], Hs[:, s["h"], u0 + c0:u0 + c1], False, True)
                c.act(p_b[k % 3][:, c0:c1], z[:, c0:c1], AF.Exp)
            else:
                if a >= q0:
                    c.mm(z[:, c0:c0 + 128], c.ident_b[:], c.maskC_b[:], False, True)
                c.act(p_b[k % 3][:, c0:c1], z[:, c0:c1], AF.Exp, bias=biasC[:, g, a, s["h"]:s["h"] + 1])

        def s2(k):
            s = steps[k]
            hh = s["h"] % 2
            pr = slice(64 * hh, 64 * hh + 64)
            a, kc, c0, c1, hv = s["a"], s["kc"], s["c0"], s["c1"], s["hv"]
            ob = c.ps[4 + 2 * hh]
            db = c.ps[5 + 2 * hh]
            vc = (hv // 2) * 128
            c.mm(ob[:, c0:c1], V[:, a, vc:vc + 128], p_b[k % 3][:, c0:c1], s["first"], s["last"])
            c.mm(db[:, c0:c1], c.ones_b[:], p_b[k % 3][:, c0:c1], s["first"], s["last"])
            if s["last"]:
                c.act(rden[pr, :], db[pr, :], AF.Ln)
                c.act(rden[pr, :], rden[pr, :], AF.Exp, scale=-1.0)
                c.tt(oT[pr, kc, :], ob[pr, :], rden[pr, :], ALU.mult)

        for k in range(n + 2):
            if k < n:
                s1(k)
            if 0 <= k - 2 < n:
                s2(k - 2)

    def merge_out(self, l, g, hTg, merged, oT, xbase, next_g=None):
        c = self
        win = c.d["w_in"][l]
        ring = list(zip(c.slots, c.slot_sems)) + [(c.arena_tile(xbase + 2048 * i, [NCH, 128], BF16), c.mslot_sems[i]) for i in range(4)]
        st = dict(ri=0)

        def nxt():
            b = ring[st["ri"] % len(ring)]
            st["ri"] += 1
            return b

        def load_cols(wl, col0):
            sl, sem = nxt()
            c.dma_in("pool", sl[:], [(sl[:], wl[:, col0:col0 + 128].rearrange("(c p) n -> p c n", p=128))], sem)
            return sl
        sig = [c.alloc([TG], F32) for _ in range(2)]
        macc = c.alloc([TG], F32)
        tmpm = c.alloc([TG], F32)
        if next_g is not None:
            nsq, nln, nrs = c.alloc_norm_tmp()
        brs = [(0, [0, 1], c.d["w_br_sb"][l]), (1, [2, 3, 4, 5], c.d["w_br_ch"][l]), (2, [6, 7], c.d["w_br_fox"][l])]
        k = 0
        for m in range(NCH):
            sl, sem = nxt()
            srcs = []
            for (bi, chunks, wd) in brs:
                n = len(chunks)
                srcs.append((sl[:, chunks[0]:chunks[0] + n, :], wd[:, m * 128:(m + 1) * 128].rearrange("(c p) n -> p c n", p=128)))
            c.dma_in("pool", sl[:], srcs, sem)
            for (bi, chunks, wd) in brs:
                sg = load_cols(win, GCOL[bi] + m * 128)
                yb = c.ps[bi]
                gb = c.ps[3 + bi]
                for i, ch in enumerate(chunks):
                    c.mm(yb[:], sl[:, ch, :], oT[:, ch, :], i == 0, i == len(chunks) - 1)
                for ch in range(NCH):
                    c.mm(gb[:], sg[:, ch, :], hTg[:, ch, :], ch == 0, ch == NCH - 1)
                s = sig[k % 2]
                k += 1
                c.act(s[:], gb[:], AF.Sigmoid, bias=c.bfm[:, l, 16 + bi * 8 + m:16 + bi * 8 + m + 1])
                if bi == 0:
                    c.tt(macc[:], yb[:], s[:], ALU.mult)
                elif bi == 1:
                    c.tt(tmpm[:], yb[:], s[:], ALU.mult)
                    c.tt(macc[:], macc[:], tmpm[:], ALU.add)
                else:
                    c.tt(tmpm[:], yb[:], s[:], ALU.mult)
                    c.tt(merged[:, m, :], macc[:], tmpm[:], ALU.add)
            if next_g is not None:
                sqv = nsq[m % 2]
                c.act(sqv[:], c.XT[:, m, next_g * TG:(next_g + 1) * TG], AF.Square)
                c.mm(c.ps[7][:], c.ones_b[:], sqv[:], m == 0, m == NCH - 1)
        if next_g is not None:
            c.act(nln[:], c.ps[7][:], AF.Ln, bias=c.eps[:, 0:1], scale=1.0 / D)
            c.act(nrs[:], nln[:], AF.Exp, scale=-0.5)
        wout = c.d["w_out"][l]
        for m in range(NCH):
            sl = load_cols(wout, m * 128)
            bank = c.ps[6 + m % 2]
            for ch in range(NCH):
                c.mm(bank[:], sl[:, ch, :], merged[:, ch, :], ch == 0, ch == NCH - 1)
            if next_g is not None:
                c.stt(hTg[:, m, :], c.XT[:, m, next_g * TG:(next_g + 1) * TG],
                      c.gT[:, (l * 3 + 1) * 8 + m:(l * 3 + 1) * 8 + m + 1], nrs[:], ALU.mult, ALU.mult)
            xv = c.XT[:, m, g * TG:(g + 1) * TG]
            c.tt(xv, bank[:], xv, ALU.add)

    def final(self, do_norm=True):
        c = self
        save = c.cur
        tmp = c.alloc_norm_tmp()
        yT = [c.alloc([TG], F32) for _ in range(NCH)]
        os_ = [c.alloc([2, D], F32) for _ in range(2)]
        sems = [c.newsem("os0"), c.newsem("os1")]
        c.out_sems = sems
        k = 0
        for tg in range(NTG):
            if do_norm:
                rstd = c.norm_stats(tg, c.ps[4 * (k % 2)], tmp)
            for ch in range(NCH):
                xv = c.XT[:, ch, tg * TG:(tg + 1) * TG]
                if do_norm:
                    c.stt(yT[ch][:], xv, c.gT[:, 48 + ch:48 + ch + 1], rstd[:], ALU.mult, ALU.mult)
                else:
                    c.copy(yT[ch][:], xv)
            for hb in range(2):
                par = k % 2
                k += 1
                banks = [c.ps[4 * par + i] for i in range(4)]
                for ch in range(NCH):
                    for bb in range(2):
                        b = 2 * hb + bb
                        bank = banks[2 * bb + ch // 4]
                        c.tr(bank[:, (ch % 4) * 128:(ch % 4 + 1) * 128], yT[ch][:, b * 128:(b + 1) * 128], c.ident_f[:])
                st = os_[par]
                for bb in range(2):
                    for hf in range(2):
                        eng = "dve" if (bb + hf) % 2 == 0 else "act"
                        c.copy(st[:, bb, hf * 512:(hf + 1) * 512], banks[2 * bb + hf][:], eng)
                r0 = tg * TG + hb * 256
                c.dma_out("sp", c.d["out"][r0:r0 + 256, :].rearrange("(b p) d -> p b d", p=128), st[:], sems[par])
        c.cur = save

    def dbg_dump(self, name, view, shape, dtype):
        if name not in self.dbg:
            return
        c = self
        t = c.nc.dram_tensor("dbg_" + name, shape, dtype, kind="ExternalOutput").ap()
        sem = c.newsem("dbg_" + name)
        c.out_sems_extra.append(sem)
        nd = len(view.ap.shape)
        src = view.ap
        if nd == 3:
            src = src.rearrange("p a b -> p (a b)")
        elif nd == 4:
            src = src.rearrange("p a b c -> p (a b c)")
        c.S.add("sp", lambda e: [e.dma_start(out=t, in_=src)], reads=[view.reg], dma=sem, ndma=1)
        c.dbg_out[name] = "dbg_" + name

    def build(self):
        nc = bass.Bass("TRN2", target_bir_lowering=False)
        self.nc = nc
        d = {}

        def din(name, shape):
            d[name] = nc.dram_tensor(name, shape, F32, kind="ExternalInput").ap()

        din("x", [T, D])
        din("w_ffn1_in", [2, D, 2 * DFF])
        din("w_ffn1_out", [2, DFF, D])
        din("w_in", [2, D, INW])
        din("w_br_sb", [2, 256, D])
        din("w_br_ch", [2, 512, D])
        din("w_br_fox", [2, 256, D])
        din("w_out", [2, D, D])
        din("w_ffn2_in", [2, D, 2 * DFF])
        din("w_ffn2_out", [2, DFF, D])
        din("gT", [128, 56])
        din("bfm", [2, 128, 40])
        din("bv", [2, D])
        din("bf", [2, 4])
        din("relx", [2, 8, EXT])
        d["out"] = nc.dram_tensor("out", [T, D], F32, kind="ExternalOutput").ap()
        self.d = d
        self.out_sems_extra = []

        self.arena_bytes = 212800
        with contextlib.ExitStack() as es:
            self.arena = es.enter_context(nc.sbuf_tensor("arena", [128, self.arena_bytes // 2], BF16))
            pst = [es.enter_context(nc.psum_tensor("ps%d" % i, [128, 512], F32)) for i in range(8)]
            self.ps = [Tile(pst[i], "ps", i * 2048, [512], 4) for i in range(8)]
            self.cur = 0
            self.peak = 0
            self.nbank = 0
            self.XT = self.alloc([NCH, T], F32)
            self.nslots = 8
            self.slots = [self.alloc([NCH, 128], BF16) for _ in range(self.nslots)]
            self.slot_sems = [self.newsem("slot%d" % i) for i in range(self.nslots)]
            self.slot_i = 0
            self.wo_sems = [self.newsem("wo%d" % i) for i in range(NF)]
            self.mslot_sems = [self.newsem("mslot%d" % i) for i in range(4)]
            self.bv_sem = self.newsem("bv")
            self.bf_sem = self.newsem("bf")
            self.wv_sem = self.newsem("wv")
            self.wf_sem = self.newsem("wf")
            self.hs_sem = self.newsem("hs")
            self.setup_consts()
            self.load_x()
            stop = self.stop_after
            done = False
            xhoist = False
            for l in range(self.n_layers):
                self.ffn(l, d["w_ffn1_in"], d["w_ffn1_out"], l * 3 + 0, skip_norm0=xhoist)
                if stop == ("ffn1", l):
                    done = True
                    break
                self.mixer(l)
                if stop == ("mix", l):
                    done = True
                    break
                xhoist = (l + 1 < self.n_layers) and stop is None
                self.ffn(l, d["w_ffn2_in"], d["w_ffn2_out"], l * 3 + 2,
                         hoist_next_gidx=((l + 1) * 3 if xhoist else None))
            self.final(do_norm=not done)
            self.S.finalize()

            sems = {}
            for e in ("pe", "act", "dve", "pool"):
                sems["eng:" + e] = es.enter_context(nc.semaphore("s_" + e))
            for name in self.dma_sems:
                sems["dma:" + name] = es.enter_context(nc.semaphore("d_" + name))
            S = self.S
            final_waits = [(sems["dma:" + s], 16 * S.dma_count[s]) for s in list(self.out_sems) + self.out_sems_extra]
            block = es.enter_context(nc.Block())

            @block.tensor
            def _(e):
                S.emit_engine("pe", e, sems)

            @block.scalar
            def _(e):
                S.emit_engine("act", e, sems)

            @block.vector
            def _(e):
                S.emit_engine("dve", e, sems)

            @block.gpsimd
            def _(e):
                S.emit_engine("pool", e, sems)

            @block.sync
            def _(e):
                S.emit_engine("sp", e, sems)
                for sem, val in final_waits:
                    e.wait_ge(sem, val)
        return nc


def prep_inputs(inputs):
    f = lambda a: np.ascontiguousarray(np.asarray(a, dtype=np.float32))
    g1, gm, g2, gf = f(inputs["g_ffn1"]), f(inputs["g_mix"]), f(inputs["g_ffn2"]), f(inputs["g_final"])
    cols = []
    for l in range(2):
        for gvec in (g1[l], gm[l], g2[l]):
            cols.append(gvec.reshape(8, 128).T)
    cols.append(gf.reshape(8, 128).T)
    gT = f(np.concatenate(cols, axis=1))
    b_in = f(inputs["b_in"])
    bfm = np.zeros((2, 128, 40), np.float32)
    for l in range(2):
        for i, c0 in enumerate(QCOL):
            bfm[l, :, i] = b_in[l, c0:c0 + 128]
        for i, c0 in enumerate(KCOL):
            bfm[l, :, 8 + i] = b_in[l, c0:c0 + 128]
        for bi in range(3):
            for m in range(8):
                c0 = GCOL[bi] + m * 128
                bfm[l, :, 16 + bi * 8 + m] = b_in[l, c0:c0 + 128]
    bv = f(np.concatenate([b_in[:, VA:VA + 256], b_in[:, VB:VB + 512], b_in[:, VC:VC + 256]], axis=1))
    bf = f(b_in[:, FOFF:FOFF + 4])
    rel = f(inputs["rel_bias"])
    idx = np.minimum(np.arange(EXT), 256)
    relx = f(np.transpose(rel, (0, 2, 1))[:, :, idx])
    shared = {
        "gT": gT, "bfm": bfm, "bv": bv, "bf": bf, "relx": relx,
    }
    for k in ("w_ffn1_in", "w_ffn1_out", "w_in", "w_br_sb", "w_br_ch", "w_br_fox", "w_out", "w_ffn2_in", "w_ffn2_out"):
        shared[k] = f(inputs[k])
    return shared


_NC_CACHE = {}


def kernel(**inputs):
    x = np.ascontiguousarray(np.asarray(inputs["x"], dtype=np.float32))
    shared = prep_inputs(inputs)
    if "nc" not in _NC_CACHE:
        _NC_CACHE["nc"] = Builder().build()
    nc = _NC_CACHE["nc"]
    in_maps = []
    for b in range(8):
        m = dict(shared)
        m["x"] = x[b]
        in_maps.append(m)
    res = run_bass_kernel_spmd(nc, in_maps, core_ids=list(range(8)))
    return np.stack([np.asarray(r["out"], dtype=np.float32) for r in res.results], axis=0)
```

```python
import contextlib
from collections import defaultdict

import numpy as np
import concourse.bass as bass
import concourse.mybir as mybir
from concourse.bass_utils import run_bass_kernel_spmd

F32 = mybir.dt.float32
BF16 = mybir.dt.bfloat16
AF = mybir.ActivationFunctionType
ALU = mybir.AluOpType

D = 1024
NCH = 8
T = 2048
TG = 512
NTG = 4
DFF = 2816
NF = 22
INW = 6148
QCOL = [0, 128, 768, 896, 1024, 1152, 2304, 2432]
KCOL = [256, 384, 1280, 1408, 1536, 1664, 2560, 2688]
VA, VB, VC = 512, 1792, 2816
FOFF = 3072
GCOL = [3076, 4100, 5124]
NEG = -30000.0
EXT = 768


class View:
    __slots__ = ("ap", "reg")

    def __init__(self, ap, reg):
        self.ap = ap
        self.reg = reg


class Tile:
    def __init__(self, ap, space, base, shape, esize):
        self.ap = ap
        self.space = space
        self.base = base
        self.shape = tuple(shape)
        self.esize = esize
        st = []
        s = 1
        for d in reversed(self.shape):
            st.append(s)
            s *= d
        self.strides = tuple(reversed(st))
        self.nbytes = s * esize

    def __getitem__(self, idx):
        if not isinstance(idx, tuple):
            idx = (idx,)
        idx = idx + (slice(None),) * (1 + len(self.shape) - len(idx))
        p = idx[0]
        p0, p1, _ = p.indices(128)
        lo = hi = 0
        for dim, stv, ix in zip(self.shape, self.strides, idx[1:]):
            if isinstance(ix, int):
                lo += ix * stv
                hi += ix * stv
            else:
                a, b, _ = ix.indices(dim)
                lo += a * stv
                hi += (b - 1) * stv
        b0 = self.base + lo * self.esize
        b1 = self.base + (hi + 1) * self.esize
        return View(self.ap[idx], (self.space, p0, p1, b0, b1))


class Sched:
    PAGE = 1024

    def __init__(self):
        self.ops = []
        self.eng_ops = defaultdict(list)
        self.pages = defaultdict(set)
        self.recs = {}
        self.nrec = 0
        self.dma_count = defaultdict(int)

    def _cands(self, reg):
        sp, p0, p1, b0, b1 = reg
        out = set()
        for pg in range(b0 // self.PAGE, (b1 - 1) // self.PAGE + 1):
            out |= self.pages.get((sp, pg), set())
        res = []
        for rid in out:
            r = self.recs[rid]
            _, q0, q1, c0, c1 = r[0]
            if q0 < p1 and p0 < q1 and c0 < b1 and b0 < c1:
                res.append(rid)
        return res

    def _newrec(self, reg, writer, readers):
        rid = self.nrec
        self.nrec += 1
        self.recs[rid] = [reg, writer, readers]
        sp, p0, p1, b0, b1 = reg
        for pg in range(b0 // self.PAGE, (b1 - 1) // self.PAGE + 1):
            self.pages[(sp, pg)].add(rid)
        return rid

    def _delrec(self, rid):
        reg = self.recs.pop(rid)[0]
        sp, p0, p1, b0, b1 = reg
        for pg in range(b0 // self.PAGE, (b1 - 1) // self.PAGE + 1):
            self.pages[(sp, pg)].discard(rid)

    @staticmethod
    def _ovl(a, b):
        return a[1] < b[2] and b[1] < a[2] and a[3] < b[4] and b[3] < a[4]

    @staticmethod
    def _contains(a, b):
        return a[1] <= b[1] and b[2] <= a[2] and a[3] <= b[3] and b[4] <= a[4]

    @staticmethod
    def _remainder(r, w):
        sp, p0, p1, b0, b1 = r
        _, q0, q1, c0, c1 = w
        out = []
        lo = max(b0, c0)
        hi = min(b1, c1)
        if b0 < lo:
            out.append((sp, p0, p1, b0, lo))
        if hi < b1:
            out.append((sp, p0, p1, hi, b1))
        if p0 < q0:
            out.append((sp, p0, min(p1, q0), lo, hi))
        if q1 < p1:
            out.append((sp, max(p0, q1), p1, lo, hi))
        return out

    def add(self, eng, fn, reads=(), writes=(), dma=None, ndma=1):
        gid = len(self.ops)
        deps = set()
        for reg in reads:
            covered = False
            for rid in self._cands(reg):
                r = self.recs[rid]
                if r[1] is not None:
                    deps.add(r[1])
                r[2].append((gid, reg))
                if self._contains(r[0], reg):
                    covered = True
            if not covered:
                self._newrec(reg, None, [(gid, reg)])
        for reg in writes:
            for rid in self._cands(reg):
                r = self.recs[rid]
                if r[1] is not None:
                    deps.add(r[1])
                for (g2, rr) in r[2]:
                    if self._ovl(rr, reg):
                        deps.add(g2)
                pieces = self._remainder(r[0], reg)
                self._delrec(rid)
                for pc in pieces:
                    self._newrec(pc, r[1], [(g2, rr) for (g2, rr) in r[2] if self._ovl(rr, pc)])
            self._newrec(reg, gid, [])
        deps.discard(gid)
        op = dict(eng=eng, fn=fn, deps=deps, dma=dma, ndma=ndma, signal=False, local=len(self.eng_ops[eng]))
        if dma is not None:
            self.dma_count[dma] += ndma
            op["dmaval"] = 16 * self.dma_count[dma]
        if eng == "pe":
            op["deps"] = {d for d in deps if not (self.ops[d]["eng"] == "pe" and self.ops[d]["dma"] is None)}
        for d in op["deps"]:
            self.ops[d]["signal"] = True
        self.ops.append(op)
        self.eng_ops[eng].append(gid)
        return gid

    def finalize(self):
        for eng, lst in self.eng_ops.items():
            cnt = 0
            for gid in lst:
                op = self.ops[gid]
                if op["dma"] is None and op["signal"]:
                    cnt += 1
                    op["sigval"] = cnt

    def emit_engine(self, eng, handle, sems):
        seen = {}
        for gid in self.eng_ops[eng]:
            op = self.ops[gid]
            need = {}
            for d in op["deps"]:
                pd = self.ops[d]
                if pd["dma"] is not None:
                    key, val = "dma:" + pd["dma"], pd["dmaval"]
                else:
                    key, val = "eng:" + pd["eng"], pd["sigval"]
                if need.get(key, 0) < val:
                    need[key] = val
            for key, val in need.items():
                if seen.get(key, 0) >= val:
                    continue
                handle.wait_ge(sems[key], val)
                seen[key] = val
            insts = op["fn"](handle)
            if op["dma"] is not None:
                if not isinstance(insts, (list, tuple)):
                    insts = [insts]
                assert len(insts) == op["ndma"]
                for ins in insts:
                    ins.then_inc(sems["dma:" + op["dma"]], 16)
            elif op["signal"]:
                if isinstance(insts, (list, tuple)):
                    insts = insts[-1]
                insts.then_inc(sems["eng:" + eng], 1)


class Builder:
    def __init__(self, n_layers=2, dbg=None, stop_after=None):
        self.n_layers = n_layers
        self.dbg = dbg or {}
        self.stop_after = stop_after
        self.S = Sched()
        self.dma_sems = []
        self.out_sems = []
        self.dbg_out = {}

    def arena_tile(self, off, shape, dtype):
        es = 4 if dtype == F32 else 2
        n = 1
        for d in shape:
            n *= d
        nbytes = n * es
        assert off % 4 == 0
        assert off + nbytes <= self.arena_bytes, (off, nbytes, self.arena_bytes)
        ap = self.arena[:, off // 2:(off + nbytes) // 2]
        if dtype == F32:
            ap = ap.bitcast(F32)
        if len(shape) == 2:
            ap = ap.rearrange("p (a b) -> p a b", a=shape[0])
        elif len(shape) == 3:
            ap = ap.rearrange("p (a b c) -> p a b c", a=shape[0], b=shape[1])
        return Tile(ap, "sb", off, shape, es)

    def alloc(self, shape, dtype):
        es = 4 if dtype == F32 else 2
        n = 1
        for d in shape:
            n *= d
        nbytes = (n * es + 31) // 32 * 32
        t = self.arena_tile(self.cur, shape, dtype)
        self.cur += nbytes
        self.peak = max(self.peak, self.cur)
        return t

    def newsem(self, name):
        self.dma_sems.append(name)
        return name

    def mm(self, out, lhsT, rhs, start, stop):
        self.S.add("pe", lambda e: e.matmul(out.ap, lhsT.ap, rhs.ap, start=start, stop=stop),
                   reads=[lhsT.reg, rhs.reg], writes=[out.reg])

    def tr(self, out, in_, ident):
        self.S.add("pe", lambda e: e.transpose(out.ap, in_.ap, ident.ap),
                   reads=[in_.reg, ident.reg], writes=[out.reg])

    def act(self, out, in_, func, bias=None, scale=1.0):
        reads = [in_.reg]
        if isinstance(bias, View):
            reads.append(bias.reg)
            b = bias.ap
        elif bias is None:
            b = 0.0
        else:
            b = bias
        self.S.add("act", lambda e: e.activation(out=out.ap, in_=in_.ap, func=func, bias=b, scale=scale),
                   reads=reads, writes=[out.reg])

    def tt(self, out, in0, in1, op, eng="dve"):
        self.S.add(eng, lambda e: e.tensor_tensor(out=out.ap, in0=in0.ap, in1=in1.ap, op=op),
                   reads=[in0.reg, in1.reg], writes=[out.reg])

    def ts(self, out, in0, s1, s2, op0, op1=None, eng="dve"):
        reads = [in0.reg]
        a1 = s1
        a2 = s2
        if isinstance(s1, View):
            reads.append(s1.reg)
            a1 = s1.ap
        if isinstance(s2, View):
            reads.append(s2.reg)
            a2 = s2.ap
        if op1 is None:
            self.S.add(eng, lambda e: e.tensor_scalar(out=out.ap, in0=in0.ap, scalar1=a1, scalar2=None, op0=op0),
                       reads=reads, writes=[out.reg])
        else:
            self.S.add(eng, lambda e: e.tensor_scalar(out=out.ap, in0=in0.ap, scalar1=a1, scalar2=a2, op0=op0, op1=op1),
                       reads=reads, writes=[out.reg])

    def stt(self, out, in0, scalar, in1, op0, op1):
        reads = [in0.reg, in1.reg]
        a = scalar
        if isinstance(scalar, View):
            reads.append(scalar.reg)
            a = scalar.ap
        self.S.add("dve", lambda e: e.scalar_tensor_tensor(out=out.ap, in0=in0.ap, scalar=a, in1=in1.ap, op0=op0, op1=op1),
                   reads=reads, writes=[out.reg])

    def copy(self, out, in_, eng="dve"):
        if eng == "act":
            self.act(out, in_, AF.Copy)
        else:
            self.S.add(eng, lambda e: e.tensor_copy(out=out.ap, in_=in_.ap), reads=[in_.reg], writes=[out.reg])

    def recip(self, out, in_):
        self.S.add("dve", lambda e: e.reciprocal(out=out.ap, in_=in_.ap), reads=[in_.reg], writes=[out.reg])

    def memset(self, out, val, eng="dve"):
        self.S.add(eng, lambda e: e.memset(out.ap, val), writes=[out.reg])

    def aselect(self, out, in_, pattern, base, cm, cmp, fill):
        self.S.add("pool", lambda e: e.affine_select(out=out.ap, in_=in_.ap, pattern=pattern, base=base,
                                                     channel_multiplier=cm, compare_op=cmp, fill=fill),
                   reads=[in_.reg], writes=[out.reg])

    def dma_in(self, queue, out, srcs, sem):
        def fn(e):
            return [e.dma_start(out=o.ap, in_=s) for o, s in srcs]
        self.S.add(queue, fn, writes=[out.reg], dma=sem, ndma=len(srcs))

    def dma_out(self, queue, dst_ap, src, sem):
        self.S.add(queue, lambda e: [e.dma_start(out=dst_ap, in_=src.ap)], reads=[src.reg], dma=sem, ndma=1)

    def slot(self):
        i = self.slot_i % self.nslots
        self.slot_i += 1
        return self.slots[i], self.slot_sems[i]

    def load_cols(self, wl, col0, ncol=128, nch=NCH):
        sl, sem = self.slot()
        v = sl[:, 0:nch, 0:ncol]
        self.dma_in("pool", v, [(v, wl[:, col0:col0 + ncol].rearrange("(c p) n -> p c n", p=128))], sem)
        return sl

    def setup_consts(self):
        c = self
        c.ident_f = c.alloc([128], F32)
        c.ones_f = c.alloc([128], F32)
        c.triu_f = c.alloc([128], F32)
        c.ident_b = c.alloc([128], BF16)
        c.J_b = c.alloc([128], BF16)
        c.ones_b = c.alloc([128], BF16)
        c.negU_b = c.alloc([128], BF16)
        c.negones_b = c.alloc([128], BF16)
        c.maskA_b = c.alloc([128], BF16)
        c.maskC_b = c.alloc([128], BF16)
        c.zeros_b = c.alloc([512], BF16)
        c.eps = c.alloc([1], F32)
        c.gT = c.alloc([56], F32)
        c.bfm = c.alloc([2, 40], F32)
        pat = [[1, 128]]
        c.memset(c.ones_f[:], 1.0, "pool")
        c.aselect(c.ident_f[:], c.ones_f[:], pat, 0, -1, ALU.is_equal, 0.0)
        c.aselect(c.triu_f[:], c.ones_f[:], pat, 0, -1, ALU.is_ge, 0.0)
        c.memset(c.ones_b[:], 1.0, "pool")
        c.memset(c.negones_b[:], -1.0, "pool")
        c.memset(c.zeros_b[:], 0.0, "pool")
        c.memset(c.eps[:], 1e-6, "pool")
        c.aselect(c.ident_b[:], c.ones_b[:], pat, 0, -1, ALU.is_equal, 0.0)
        c.aselect(c.J_b[:], c.ones_b[:], pat, -127, 1, ALU.is_equal, 0.0)
        c.aselect(c.negU_b[:], c.negones_b[:], [[-1, 128]], 0, 1, ALU.is_ge, 0.0)
        c.aselect(c.maskA_b[:], c.zeros_b[:, 0:128], pat, 0, -1, ALU.is_gt, NEG)
        c.aselect(c.maskC_b[:], c.zeros_b[:, 0:128], pat, 0, -1, ALU.is_ge, NEG)
        sem = c.newsem("vec")
        c.dma_in("sp", c.gT[:], [(c.gT[:], c.d["gT"])], sem)
        sem = c.newsem("bfm")
        c.dma_in("sp", c.bfm[:], [(c.bfm[:], c.d["bfm"].rearrange("l p n -> p l n"))], sem)

    def load_x(self):
        c = self
        save = c.cur
        xs = [c.alloc([4, D], F32) for _ in range(2)]
        sems = [c.newsem("xs0"), c.newsem("xs1")]
        k = 0
        for tg in range(NTG):
            st = xs[tg % 2]
            c.dma_in("sp", st[:], [(st[:], c.d["x"][tg * TG:(tg + 1) * TG, :].rearrange("(b p) d -> p b d", p=128))], sems[tg % 2])
            for ch in range(NCH):
                bank = c.ps[k % 8]
                for b in range(4):
                    c.tr(bank[:, b * 128:(b + 1) * 128], st[:, b, ch * 128:(ch + 1) * 128], c.ident_f[:])
                dst = c.XT[:, ch, tg * TG:(tg + 1) * TG]
                if k % 2 == 0:
                    c.copy(dst, bank[:], "dve")
                else:
                    c.copy(dst, bank[:], "act")
                k += 1
        c.cur = save

    def norm_stats(self, tg, bank, tmp):
        c = self
        sq, lnt, rstd = tmp
        for ch in range(NCH):
            s = sq[ch % 2]
            c.act(s[:], c.XT[:, ch, tg * TG:(tg + 1) * TG], AF.Square)
            c.mm(bank[:], c.ones_b[:], s[:], ch == 0, ch == NCH - 1)
        c.act(lnt[:], bank[:], AF.Ln, bias=c.eps[:, 0:1], scale=1.0 / D)
        c.act(rstd[:], lnt[:], AF.Exp, scale=-0.5)
        return rstd

    def norm_to(self, tg, gidx, dst_fn, bank, tmp):
        c = self
        rstd = c.norm_stats(tg, bank, tmp)
        for ch in range(NCH):
            c.stt(dst_fn(ch), c.XT[:, ch, tg * TG:(tg + 1) * TG], c.gT[:, gidx * 8 + ch:gidx * 8 + ch + 1], rstd[:],
                  ALU.mult, ALU.mult)

    def alloc_norm_tmp(self):
        c = self
        return ([c.alloc([TG], BF16), c.alloc([TG], BF16)], c.alloc([TG], F32), c.alloc([TG], F32))

    def ffn(self, l, w_in_d, w_out_d, gidx):
        c = self
        save = c.cur
        hT = c.alloc([NCH, 2 * TG], BF16)
        actT = c.alloc([NF, 2 * TG], BF16)
        WO = c.alloc([NF, D], BF16)
        tmp = c.alloc_norm_tmp()
        sqx = list(tmp[0]) + [c.alloc([TG], BF16) for _ in range(6)]
        sil = [c.alloc([TG], F32) for _ in range(2)]
        big = [(c.arena_tile(c.slots[2 * i].base, [NCH, 256], BF16), c.slot_sems[2 * i]) for i in range(4)]
        wl_in = w_in_d[l]
        wl_out = w_out_d[l]
        st = dict(ri=0)

        def load2(col0):
            sl, sem = big[st["ri"] % 4]
            st["ri"] += 1
            c.dma_in("pool", sl[:], [(sl[:], wl_in[:, col0:col0 + 256].rearrange("(c p) n -> p c n", p=128))], sem)
            return sl

        def hoisted_norm_piece(m):
            rs = [tmp[1], tmp[2]]
            if m in (0, 1):
                tg = 2 + m
                for ch in range(NCH):
                    c.act(sqx[ch][:], c.XT[:, ch, tg * TG:(tg + 1) * TG], AF.Square)
                    c.mm(c.ps[6 + m][:], c.ones_b[:], sqx[ch][:], ch == 0, ch == NCH - 1)
            elif m == 2:
                for t2 in range(2):
                    c.act(rs[t2][:], c.ps[6 + t2][:], AF.Ln, bias=c.eps[:, 0:1], scale=1.0 / D)
                    c.act(rs[t2][:], rs[t2][:], AF.Exp, scale=-0.5)
            elif m <= 6:
                for i in range(4):
                    idx = (m - 3) * 4 + i
                    t2, ch = idx // NCH, idx % NCH
                    tg = 2 + t2
                    c.stt(hT[:, ch, t2 * TG:(t2 + 1) * TG], c.XT[:, ch, tg * TG:(tg + 1) * TG],
                          c.gT[:, gidx * 8 + ch:gidx * 8 + ch + 1], rs[t2][:], ALU.mult, ALU.mult)

        for half in range(2):
            if half == 0:
                for t2 in range(2):
                    tg = 2 * half + t2
                    c.norm_to(tg, gidx, lambda ch, t2=t2: hT[:, ch, t2 * TG:(t2 + 1) * TG], c.ps[(c.nbank) % 8], tmp)
                    c.nbank += 1
            for jp in range(NF // 2):
                sg = load2(jp * 256)
                su = load2(DFF + jp * 256)
                if half == 0:
                    for f in (2 * jp, 2 * jp + 1):
                        c.dma_in("pool", WO[:, f, :], [(WO[:, f, :], wl_out[f * 128:(f + 1) * 128, :])], c.wo_sems[f])
                for sub in range(2):
                    j = 2 * jp + sub
                    par = j % 2
                    cs = slice(sub * 128, (sub + 1) * 128)
                    gb = [c.ps[4 * par + 0], c.ps[4 * par + 1]]
                    ub = [c.ps[4 * par + 2], c.ps[4 * par + 3]]
                    for ch in range(NCH):
                        for t2 in range(2):
                            c.mm(gb[t2][:], sg[:, ch, cs], hT[:, ch, t2 * TG:(t2 + 1) * TG], ch == 0, ch == NCH - 1)
                    for ch in range(NCH):
                        for t2 in range(2):
                            c.mm(ub[t2][:], su[:, ch, cs], hT[:, ch, t2 * TG:(t2 + 1) * TG], ch == 0, ch == NCH - 1)
                    for t2 in range(2):
                        c.act(sil[t2][:], gb[t2][:], AF.Silu)
                        c.tt(actT[:, j, t2 * TG:(t2 + 1) * TG], sil[t2][:], ub[t2][:], ALU.mult)
            for m in range(NCH):
                bk = [c.ps[2 * (m % 2)], c.ps[2 * (m % 2) + 1]]
                for f in range(NF):
                    for t2 in range(2):
                        c.mm(bk[t2][:], WO[:, f, m * 128:(m + 1) * 128], actT[:, f, t2 * TG:(t2 + 1) * TG], f == 0, f == NF - 1)
                if half == 0:
                    hoisted_norm_piece(m)
                for t2 in range(2):
                    tg = 2 * half + t2
                    xv = c.XT[:, m, tg * TG:(tg + 1) * TG]
                    c.stt(xv, bk[t2][:], 0.5, xv, ALU.mult, ALU.add)
        c.cur = save

    def mixer(self, l):
        c = self
        save = c.cur
        win = c.d["w_in"][l]
        KT = c.alloc([NCH, T], BF16)
        V = c.alloc([16, D], BF16)
        Hs = c.alloc([8, 640], BF16)
        biasC = c.alloc([4, 16, 4], F32)
        base2 = c.cur
        bv = c.alloc([D], F32)
        c.dma_in("sp", bv[:], [(bv[:], c.d["bv"][l:l + 1, :].partition_broadcast(128))], c.bv_sem)
        bfx = c.alloc([16, 4], F32)
        nlf = c.alloc([16, 4], F32)
        totp = c.alloc([16, 4], F32)
        ncp = c.alloc([16, 4], F32)
        nfc = c.alloc([16, 4], F32)
        nncp = c.alloc([16, 4], F32)
        c.dma_in("sp", bfx[:], [(bfx[:], bass.AP(c.d["bf"].tensor, l * 4, [[0, 128], [0, 16], [1, 4]]))], c.bf_sem)
        hT = c.alloc([NCH, 2 * TG], BF16)
        Wv = c.alloc([NCH, D], BF16)
        Wf = c.alloc([NCH, 4], BF16)
        tmp = c.alloc_norm_tmp()
        srcs = []
        for (vc, n, o) in ((VA, 256, 0), (VB, 512, 256), (VC, 256, 768)):
            srcs.append((Wv[:, :, o:o + n], win[:, vc:vc + n].rearrange("(c p) n -> p c n", p=128)))
        c.dma_in("pool", Wv[:], srcs, c.wv_sem)
        c.dma_in("pool", Wf[:], [(Wf[:], win[:, FOFF:FOFF + 4].rearrange("(c p) n -> p c n", p=128))], c.wf_sem)
        psf = c.ps[7]
        for half in range(2):
            for t2 in range(2):
                tg = 2 * half + t2
                c.norm_to(tg, l * 3 + 1, lambda ch, t2=t2: hT[:, ch, t2 * TG:(t2 + 1) * TG], c.ps[6], tmp)
            for kc in range(8):
                sl = c.load_cols(win, KCOL[kc])
                bk = [c.ps[2 * (kc % 2)], c.ps[2 * (kc % 2) + 1]]
                for ch in range(NCH):
                    for t2 in range(2):
                        c.mm(bk[t2][:], sl[:, ch, :], hT[:, ch, t2 * TG:(t2 + 1) * TG], ch == 0, ch == NCH - 1)
                for t2 in range(2):
                    tg = 2 * half + t2
                    dst = KT[:, kc, tg * TG:(tg + 1) * TG]
                    bcol = c.bfm[:, l, 8 + kc:8 + kc + 1]
                    if t2 == 0:
                        c.ts(dst, bk[t2][:], bcol, None, ALU.add)
                    else:
                        c.act(dst, bk[t2][:], AF.Identity, bias=bcol)
            for blk in range(8):
                ab = 8 * half + blk
                for cg in range(2):
                    bank = c.ps[4 + (2 * blk + cg) % 2]
                    for ch in range(NCH):
                        c.mm(bank[:], hT[:, ch, blk * 128:(blk + 1) * 128], Wv[:, ch, cg * 512:(cg + 1) * 512], ch == 0, ch == NCH - 1)
                    c.tt(V[:, ab, cg * 512:(cg + 1) * 512], bank[:], bv[:, cg * 512:(cg + 1) * 512], ALU.add)
                for ch in range(NCH):
                    c.mm(psf[:, ab * 4:ab * 4 + 4], hT[:, ch, blk * 128:(blk + 1) * 128], Wf[:, ch, :], ch == 0, ch == NCH - 1)
        flat = lambda t: View(t.ap.rearrange("p a b -> p (a b)"), t[:].reg)
        c.tt(flat(totp), psf[:, 0:64], flat(bfx), ALU.add)
        c.act(flat(nfc), flat(totp), AF.Exp, scale=-1.0)
        c.ts(flat(totp), flat(nfc), 1.0, None, ALU.add)
        c.act(flat(nlf), flat(totp), AF.Ln)
        pt = c.ps[6]
        c.mm(pt[:, 64:128], c.triu_f[:], flat(nlf), True, True)
        for b in range(15):
            n = 15 - b
            outv = View(pt.ap[:, (b + 1) * 4:64].rearrange("p (a b) -> p a b", b=4), pt[:, (b + 1) * 4:64].reg)
            rhsv = View(nlf.ap[:, b:b + 1, :].broadcast_to([128, n, 4]), nlf[:, b, :].reg)
            c.mm(outv, c.ones_f[:], rhsv, b == 0, b == 14)
        c.memset(ncp[:, 0, :], 0.0)
        c.memset(nncp[:, 0, :], 0.0)
        pt3 = pt.ap[:, 4:64].rearrange("p (a b) -> p a b", b=4)
        c.S.add("act", lambda e: e.activation(out=ncp.ap[:, 1:16, :], in_=pt3, func=AF.Identity),
                reads=[pt[:, 4:64].reg], writes=[ncp[:, 1:16, :].reg])
        c.S.add("act", lambda e: e.activation(out=nncp.ap[:, 1:16, :], in_=pt3, func=AF.Identity, scale=-1.0),
                reads=[pt[:, 4:64].reg], writes=[nncp[:, 1:16, :].reg])
        c.tt(flat(nfc), pt[:, 64:128], flat(ncp), ALU.add)
        for g in range(NTG):
            na = 4 * g + 4
            for h in range(4):
                c.act(biasC[:, g, 0:na, h], nfc[:, 0:na, h], AF.Identity, bias=nncp[:, 4 * g, h:h + 1])
        c.dbg_dump("KT%d" % l, KT[:], [128, NCH * T], BF16)
        c.dbg_dump("V%d" % l, V[:], [128, 16 * D], BF16)
        c.dbg_dump("biasC%d" % l, biasC[:], [128, 256], F32)
        c.cur = base2
        c.dma_in("pool", Hs[:], [(Hs[:], bass.AP(c.d["relx"].tensor, l * 8 * EXT + 1, [[1, 128], [EXT, 8], [1, 640]]))], c.hs_sem)
        c.memset(Hs[64:128, :, 576:640], NEG)
        c.memset(Hs[0:64, :, 0:64], NEG)
        hTg = c.alloc([NCH, TG], BF16)
        qT = c.alloc([NCH, 2, TG], BF16)
        merged = c.arena_tile(qT.base, [NCH, TG], BF16)
        oT = c.alloc([NCH, TG], BF16)
        base3 = c.cur
        for g in range(NTG):
            c.cur = base3
            if g == 0:
                tmp = c.alloc_norm_tmp()
                c.norm_to(g, l * 3 + 1, lambda ch: hTg[:, ch, :], c.ps[7], tmp)
            c.memset(qT[64:128, :, 0, :], 0.0)
            c.memset(qT[0:64, :, 1, :], 0.0)
            for qc in range(8):
                if qc % 2 == 0:
                    bi = (qc // 2) % 4
                    qbig = c.arena_tile(c.slots[2 * bi].base, [NCH, 256], BF16)
                    c.dma_in("pool", qbig[:], [(qbig[:], win[:, QCOL[qc]:QCOL[qc] + 256].rearrange("(c p) n -> p c n", p=128))],
                             c.slot_sems[2 * bi])
                cs = slice((qc % 2) * 128, (qc % 2) * 128 + 128)
                bank = c.ps[qc % 2]
                for ch in range(NCH):
                    c.mm(bank[:], qbig[:, ch, cs], hTg[:, ch, :], ch == 0, ch == NCH - 1)
                c.ts(qT[0:64, qc, 0, :], bank[0:64, :], c.bfm[0:64, l, qc:qc + 1], 0.125, ALU.add, ALU.mult)
                c.ts(qT[64:128, qc, 1, :], bank[64:128, :], c.bfm[64:128, l, qc:qc + 1], 0.125, ALU.add, ALU.mult)
            c.cur = base3
            c.attn_A(g, KT, V, qT, oT)
            c.cur = base3
            c.attn_BC(g, KT, V, qT, oT, Hs, biasC)
            if g == 0:
                pass
                c.dbg_dump("oT%d" % l, oT[:], [128, NCH * TG], BF16)
            c.cur = base3
            c.merge_out(l, g, hTg, merged, oT, qT.base + 8192, (g + 1) if g + 1 < NTG else None)
        c.cur = save

    def attn_A(self, g, KT, V, qT, oT):
        c = self
        e_sb = c.alloc([TG], F32)
        sp_b = [c.alloc([TG], BF16) for _ in range(2)]
        R = c.alloc([TG], F32)
        Rb = [c.alloc([TG], BF16) for _ in range(3)]
        w_b = [c.alloc([TG], BF16) for _ in range(2)]
        steps = []
        for h in range(4):
            for a in reversed(range(4 * g + 4)):
                steps.append((h, a))
        q0 = 4 * g
        n = len(steps)

        def geom(h, a):
            col0 = max(0, a - q0) * 128
            hh = h % 2
            pr = slice(64 * hh, 64 * hh + 64)
            kc = h // 2
            return col0, pr, kc

        def zmm(bank, h, a, stop):
            col0, pr, kc = geom(h, a)
            diag = a >= q0
            c.mm(bank[:, col0:TG], KT[:, kc, a * 128:(a + 1) * 128], qT[:, kc, h % 2, col0:TG], True, stop and not diag)
            if diag:
                c.mm(bank[:, col0:col0 + 128], c.ident_b[:], c.maskA_b[:], False, stop)

        def s1(k):
            h, a = steps[k]
            col0, pr, kc = geom(h, a)
            first = a == 4 * g + 3
            last = a == 0
            z1 = c.ps[k % 2]
            zmm(z1, h, a, True)
            c.act(e_sb[:, col0:TG], z1[:, col0:TG], AF.Exp)
            c.act(sp_b[k % 2][:, col0:TG], e_sb[:, col0:TG], AF.Ln, bias=1.0)
            if not last:
                r = R
                if first:
                    c.memset(r[:], 0.0)
                c.tt(r[:, col0:TG], r[:, col0:TG], sp_b[k % 2][:, col0:TG], ALU.add)
                c.copy(Rb[(k + 1) % 3][:, col0:TG], r[:, col0:TG])

        def s2(k):
            h, a = steps[k]
            col0, pr, kc = geom(h, a)
            first = a == 4 * g + 3
            z2 = c.ps[2 + k % 2]
            zmm(z2, h, a, False)
            colR = max(0, a + 1 - q0) * 128
            hasR = (not first) and colR < TG
            c.mm(z2[:, col0:TG], c.negU_b[:], sp_b[k % 2][:, col0:TG], False, not hasR)
            if hasR:
                c.mm(z2[:, colR:TG], c.negones_b[:], Rb[k % 3][:, colR:TG], False, True)
            c.act(w_b[k % 2][:, col0:TG], z2[:, col0:TG], AF.Exp)

        def s3(k):
            h, a = steps[k]
            col0, pr, kc = geom(h, a)
            first = a == 4 * g + 3
            last = a == 0
            ob = c.ps[4 + 2 * (h % 2)]
            if first:
                c.mm(ob[:, :], c.zeros_b[:, 0:128], c.zeros_b[:], True, False)
            c.mm(ob[:, col0:TG], V[:, a, kc * 128:(kc + 1) * 128], w_b[k % 2][:, col0:TG], False, last)
            if last:
                c.copy(oT[pr, kc, :], ob[pr, :], "dve")

        for k in range(n + 2):
            if k < n:
                s1(k)
            if 0 <= k - 1 < n:
                s2(k - 1)
            if 0 <= k - 2 < n:
                s3(k - 2)

    def attn_BC(self, g, KT, V, qT, oT, Hs, biasC):
        c = self
        p_b = [c.alloc([TG], BF16) for _ in range(3)]
        rden = c.alloc([TG], F32)
        q0 = 4 * g
        steps = []
        for h in range(8):
            first = q0 - 1 if g > 0 else 0
            alist = [first] + [a for a in range(max(0, q0 - 4), q0 + 4) if a != first]
            for i, a in enumerate(alist):
                u0 = 128 * (q0 - a)
                c0 = max(0, -u0)
                c1 = min(TG, 640 - u0)
                steps.append(dict(kind="B", h=h, a=a, kc=2 + h // 2, hv=4 + h, c0=c0, c1=c1, u0=u0,
                                  first=(i == 0), last=(i == len(alist) - 1)))
        for h in range(4):
            na = q0 + 4
            for a in range(na):
                col0 = max(0, a - q0) * 128
                steps.append(dict(kind="C", h=h, a=a, kc=6 + h // 2, hv=12 + h, c0=col0, c1=TG,
                                  first=(a == 0), last=(a == na - 1)))
        n = len(steps)

        def s1(k):
            s = steps[k]
            hh = s["h"] % 2
            pr = slice(64 * hh, 64 * hh + 64)
            a, kc, c0, c1 = s["a"], s["kc"], s["c0"], s["c1"]
            z = c.ps[k % 4]
            c.mm(z[:, c0:c1], KT[:, kc, a * 128:(a + 1) * 128], qT[:, kc, hh, c0:c1], True,
                 s["kind"] == "C" and a < q0)
            if s["kind"] == "B":
                u0 = s["u0"]
                c.mm(z[:, c0:c1], c.J_b[:], Hs[:, s["h"], u0 + c0:u0 + c1], False, True)
                c.act(p_b[k % 3][:, c0:c1], z[:, c0:c1], AF.Exp)
            else:
                if a >= q0:
                    c.mm(z[:, c0:c0 + 128], c.ident_b[:], c.maskC_b[:], False, True)
                c.act(p_b[k % 3][:, c0:c1], z[:, c0:c1], AF.Exp, bias=biasC[:, g, a, s["h"]:s["h"] + 1])

        def s2(k):
            s = steps[k]
            hh = s["h"] % 2
            pr = slice(64 * hh, 64 * hh + 64)
            a, kc, c0, c1, hv = s["a"], s["kc"], s["c0"], s["c1"], s["hv"]
            ob = c.ps[4 + 2 * hh]
            db = c.ps[5 + 2 * hh]
            vc = (hv // 2) * 128
            c.mm(ob[:, c0:c1], V[:, a, vc:vc + 128], p_b[k % 3][:, c0:c1], s["first"], s["last"])
            c.mm(db[:, c0:c1], c.ones_b[:], p_b[k % 3][:, c0:c1], s["first"], s["last"])
            if s["last"]:
                c.act(rden[pr, :], db[pr, :], AF.Ln)
                c.act(rden[pr, :], rden[pr, :], AF.Exp, scale=-1.0)
                c.tt(oT[pr, kc, :], ob[pr, :], rden[pr, :], ALU.mult)

        for k in range(n + 2):
            if k < n:
                s1(k)
            if 0 <= k - 2 < n:
                s2(k - 2)

    def merge_out(self, l, g, hTg, merged, oT, xbase, next_g=None):
        c = self
        win = c.d["w_in"][l]
        ring = list(zip(c.slots, c.slot_sems)) + [(c.arena_tile(xbase + 2048 * i, [NCH, 128], BF16), c.mslot_sems[i]) for i in range(4)]
        st = dict(ri=0)

        def nxt():
            b = ring[st["ri"] % len(ring)]
            st["ri"] += 1
            return b

        def load_cols(wl, col0):
            sl, sem = nxt()
            c.dma_in("pool", sl[:], [(sl[:], wl[:, col0:col0 + 128].rearrange("(c p) n -> p c n", p=128))], sem)
            return sl
        sig = [c.alloc([TG], F32) for _ in range(2)]
        macc = c.alloc([TG], F32)
        tmpm = c.alloc([TG], F32)
        if next_g is not None:
            nsq, nln, nrs = c.alloc_norm_tmp()
        brs = [(0, [0, 1], c.d["w_br_sb"][l]), (1, [2, 3, 4, 5], c.d["w_br_ch"][l]), (2, [6, 7], c.d["w_br_fox"][l])]
        k = 0
        for m in range(NCH):
            sl, sem = nxt()
            srcs = []
            for (bi, chunks, wd) in brs:
                n = len(chunks)
                srcs.append((sl[:, chunks[0]:chunks[0] + n, :], wd[:, m * 128:(m + 1) * 128].rearrange("(c p) n -> p c n", p=128)))
            c.dma_in("pool", sl[:], srcs, sem)
            for (bi, chunks, wd) in brs:
                sg = load_cols(win, GCOL[bi] + m * 128)
                yb = c.ps[bi]
                gb = c.ps[3 + bi]
                for i, ch in enumerate(chunks):
                    c.mm(yb[:], sl[:, ch, :], oT[:, ch, :], i == 0, i == len(chunks) - 1)
                for ch in range(NCH):
                    c.mm(gb[:], sg[:, ch, :], hTg[:, ch, :], ch == 0, ch == NCH - 1)
                s = sig[k % 2]
                k += 1
                c.act(s[:], gb[:], AF.Sigmoid, bias=c.bfm[:, l, 16 + bi * 8 + m:16 + bi * 8 + m + 1])
                if bi == 0:
                    c.tt(macc[:], yb[:], s[:], ALU.mult)
                elif bi == 1:
                    c.tt(tmpm[:], yb[:], s[:], ALU.mult)
                    c.tt(macc[:], macc[:], tmpm[:], ALU.add)
                else:
                    c.tt(tmpm[:], yb[:], s[:], ALU.mult)
                    c.tt(merged[:, m, :], macc[:], tmpm[:], ALU.add)
            if next_g is not None:
                sqv = nsq[m % 2]
                c.act(sqv[:], c.XT[:, m, next_g * TG:(next_g + 1) * TG], AF.Square)
                c.mm(c.ps[7][:], c.ones_b[:], sqv[:], m == 0, m == NCH - 1)
        if next_g is not None:
            c.act(nln[:], c.ps[7][:], AF.Ln, bias=c.eps[:, 0:1], scale=1.0 / D)
            c.act(nrs[:], nln[:], AF.Exp, scale=-0.5)
        wout = c.d["w_out"][l]
        for m in range(NCH):
            sl = load_cols(wout, m * 128)
            bank = c.ps[6 + m % 2]
            for ch in range(NCH):
                c.mm(bank[:], sl[:, ch, :], merged[:, ch, :], ch == 0, ch == NCH - 1)
            if next_g is not None:
                c.stt(hTg[:, m, :], c.XT[:, m, next_g * TG:(next_g + 1) * TG],
                      c.gT[:, (l * 3 + 1) * 8 + m:(l * 3 + 1) * 8 + m + 1], nrs[:], ALU.mult, ALU.mult)
            xv = c.XT[:, m, g * TG:(g + 1) * TG]
            c.tt(xv, bank[:], xv, ALU.add)

    def final(self, do_norm=True):
        c = self
        save = c.cur
        tmp = c.alloc_norm_tmp()
        yT = [c.alloc([TG], F32) for _ in range(NCH)]
        os_ = [c.alloc([2, D], F32) for _ in range(2)]
        sems = [c.newsem("os0"), c.newsem("os1")]
        c.out_sems = sems
        k = 0
        for tg in range(NTG):
            if do_norm:
                rstd = c.norm_stats(tg, c.ps[4 * (k % 2)], tmp)
            for ch in range(NCH):
                xv = c.XT[:, ch, tg * TG:(tg + 1) * TG]
                if do_norm:
                    c.stt(yT[ch][:], xv, c.gT[:, 48 + ch:48 + ch + 1], rstd[:], ALU.mult, ALU.mult)
                else:
                    c.copy(yT[ch][:], xv)
            for hb in range(2):
                par = k % 2
                k += 1
                banks = [c.ps[4 * par + i] for i in range(4)]
                for ch in range(NCH):
                    for bb in range(2):
                        b = 2 * hb + bb
                        bank = banks[2 * bb + ch // 4]
                        c.tr(bank[:, (ch % 4) * 128:(ch % 4 + 1) * 128], yT[ch][:, b * 128:(b + 1) * 128], c.ident_f[:])
                st = os_[par]
                for bb in range(2):
                    for hf in range(2):
                        eng = "dve" if (bb + hf) % 2 == 0 else "act"
                        c.copy(st[:, bb, hf * 512:(hf + 1) * 512], banks[2 * bb + hf][:], eng)
                r0 = tg * TG + hb * 256
                c.dma_out("sp", c.d["out"][r0:r0 + 256, :].rearrange("(b p) d -> p b d", p=128), st[:], sems[par])
        c.cur = save

    def dbg_dump(self, name, view, shape, dtype):
        if name not in self.dbg:
            return
        c = self
        t = c.nc.dram_tensor("dbg_" + name, shape, dtype, kind="ExternalOutput").ap()
        sem = c.newsem("dbg_" + name)
        c.out_sems_extra.append(sem)
        nd = len(view.ap.shape)
        src = view.ap
        if nd == 3:
            src = src.rearrange("p a b -> p (a b)")
        elif nd == 4:
            src = src.rearrange("p a b c -> p (a b c)")
        c.S.add("sp", lambda e: [e.dma_start(out=t, in_=src)], reads=[view.reg], dma=sem, ndma=1)
        c.dbg_out[name] = "dbg_" + name

    def build(self):
        nc = bass.Bass("TRN2", target_bir_lowering=False)
        self.nc = nc
        d = {}

        def din(name, shape):
            d[name] = nc.dram_tensor(name, shape, F32, kind="ExternalInput").ap()

        din("x", [T, D])
        din("w_ffn1_in", [2, D, 2 * DFF])
        din("w_ffn1_out", [2, DFF, D])
        din("w_in", [2, D, INW])
        din("w_br_sb", [2, 256, D])
        din("w_br_ch", [2, 512, D])
        din("w_br_fox", [2, 256, D])
        din("w_out", [2, D, D])
        din("w_ffn2_in", [2, D, 2 * DFF])
        din("w_ffn2_out", [2, DFF, D])
        din("gT", [128, 56])
        din("bfm", [2, 128, 40])
        din("bv", [2, D])
        din("bf", [2, 4])
        din("relx", [2, 8, EXT])
        d["out"] = nc.dram_tensor("out", [T, D], F32, kind="ExternalOutput").ap()
        self.d = d
        self.out_sems_extra = []

        self.arena_bytes = 212800
        with contextlib.ExitStack() as es:
            self.arena = es.enter_context(nc.sbuf_tensor("arena", [128, self.arena_bytes // 2], BF16))
            pst = [es.enter_context(nc.psum_tensor("ps%d" % i, [128, 512], F32)) for i in range(8)]
            self.ps = [Tile(pst[i], "ps", i * 2048, [512], 4) for i in range(8)]
            self.cur = 0
            self.peak = 0
            self.nbank = 0
            self.XT = self.alloc([NCH, T], F32)
            self.nslots = 8
            self.slots = [self.alloc([NCH, 128], BF16) for _ in range(self.nslots)]
            self.slot_sems = [self.newsem("slot%d" % i) for i in range(self.nslots)]
            self.slot_i = 0
            self.wo_sems = [self.newsem("wo%d" % i) for i in range(NF)]
            self.mslot_sems = [self.newsem("mslot%d" % i) for i in range(4)]
            self.bv_sem = self.newsem("bv")
            self.bf_sem = self.newsem("bf")
            self.wv_sem = self.newsem("wv")
            self.wf_sem = self.newsem("wf")
            self.hs_sem = self.newsem("hs")
            self.setup_consts()
            self.load_x()
            stop = self.stop_after
            done = False
            for l in range(self.n_layers):
                self.ffn(l, d["w_ffn1_in"], d["w_ffn1_out"], l * 3 + 0)
                if stop == ("ffn1", l):
                    done = True
                    break
                self.mixer(l)
                if stop == ("mix", l):
                    done = True
                    break
                self.ffn(l, d["w_ffn2_in"], d["w_ffn2_out"], l * 3 + 2)
            self.final(do_norm=not done)
            self.S.finalize()

            sems = {}
            for e in ("pe", "act", "dve", "pool"):
                sems["eng:" + e] = es.enter_context(nc.semaphore("s_" + e))
            for name in self.dma_sems:
                sems["dma:" + name] = es.enter_context(nc.semaphore("d_" + name))
            S = self.S
            final_waits = [(sems["dma:" + s], 16 * S.dma_count[s]) for s in list(self.out_sems) + self.out_sems_extra]
            block = es.enter_context(nc.Block())

            @block.tensor
            def _(e):
                S.emit_engine("pe", e, sems)

            @block.scalar
            def _(e):
                S.emit_engine("act", e, sems)

            @block.vector
            def _(e):
                S.emit_engine("dve", e, sems)

            @block.gpsimd
            def _(e):
                S.emit_engine("pool", e, sems)

            @block.sync
            def _(e):
                S.emit_engine("sp", e, sems)
                for sem, val in final_waits:
                    e.wait_ge(sem, val)
        return nc


def prep_inputs(inputs):
    f = lambda a: np.ascontiguousarray(np.asarray(a, dtype=np.float32))
    g1, gm, g2, gf = f(inputs["g_ffn1"]), f(inputs["g_mix"]), f(inputs["g_ffn2"]), f(inputs["g_final"])
    cols = []
    for l in range(2):
        for gvec in (g1[l], gm[l], g2[l]):
            cols.append(gvec.reshape(8, 128).T)
    cols.append(gf.reshape(8, 128).T)
    gT = f(np.concatenate(cols, axis=1))
    b_in = f(inputs["b_in"])
    bfm = np.zeros((2, 128, 40), np.float32)
    for l in range(2):
        for i, c0 in enumerate(QCOL):
            bfm[l, :, i] = b_in[l, c0:c0 + 128]
        for i, c0 in enumerate(KCOL):
            bfm[l, :, 8 + i] = b_in[l, c0:c0 + 128]
        for bi in range(3):
            for m in range(8):
                c0 = GCOL[bi] + m * 128
                bfm[l, :, 16 + bi * 8 + m] = b_in[l, c0:c0 + 128]
    bv = f(np.concatenate([b_in[:, VA:VA + 256], b_in[:, VB:VB + 512], b_in[:, VC:VC + 256]], axis=1))
    bf = f(b_in[:, FOFF:FOFF + 4])
    rel = f(inputs["rel_bias"])
    idx = np.minimum(np.arange(EXT), 256)
    relx = f(np.transpose(rel, (0, 2, 1))[:, :, idx])
    shared = {
        "gT": gT, "bfm": bfm, "bv": bv, "bf": bf, "relx": relx,
    }
    for k in ("w_ffn1_in", "w_ffn1_out", "w_in", "w_br_sb", "w_br_ch", "w_br_fox", "w_out", "w_ffn2_in", "w_ffn2_out"):
        shared[k] = f(inputs[k])
    return shared


_NC_CACHE = {}


def kernel(**inputs):
    x = np.ascontiguousarray(np.asarray(inputs["x"], dtype=np.float32))
    shared = prep_inputs(inputs)
    if "nc" not in _NC_CACHE:
        _NC_CACHE["nc"] = Builder().build()
    nc = _NC_CACHE["nc"]
    in_maps = []
    for b in range(8):
        m = dict(shared)
        m["x"] = x[b]
        in_maps.append(m)
    res = run_bass_kernel_spmd(nc, in_maps, core_ids=list(range(8)))
    return np.stack([np.asarray(r["out"], dtype=np.float32) for r in res.results], axis=0)
```
